# Optimizing a Trainium2 kernel written in Bass

```python
import math
import jax, jax.numpy as jnp
from jax import lax
import numpy as np

D_MODEL = 4096
BATCH = 2
SEQ = 4096
DEPTH = 1

MEM_LEN = 256
MIX_WIDTH = D_MODEL
MLSTM_HEADS = 4
MLSTM_V = MIX_WIDTH // 2
MLSTM_DV = MLSTM_V // MLSTM_HEADS
MLSTM_DQK = MLSTM_DV // 2
MLSTM_QK = MLSTM_HEADS * MLSTM_DQK
MLSTM_CHUNK = 64
GATE_CAP = 15.0
CONV_WIDTH = MIX_WIDTH // 2
CONV_K = 3
N_BRANCH = 2
XATTN_HEADS = 4
XATTN_DH = D_MODEL // XATTN_HEADS
D_FF = ((8 * D_MODEL // 3 + 255) // 256) * 256
IN_WIDTH = 2 * MLSTM_QK + 2 * MLSTM_V + 2 * MLSTM_HEADS + 3 * CONV_WIDTH + N_BRANCH * D_MODEL
EPS = 1e-6

kernel_name = "hybrid_mlstm_shortconv_gated_block"


def rms_norm(x, w):
    xf = x.astype(jnp.float32)
    y = xf * lax.rsqrt(jnp.mean(xf * xf, axis=-1, keepdims=True) + EPS)
    return (y * w.astype(jnp.float32)).astype(x.dtype)


def mlstm_chunkwise(q, k, v, i_pre, f_pre):
    out_dtype = v.dtype
    B, S, H, dqk = q.shape
    dv = v.shape[-1]
    L = MLSTM_CHUNK
    nc = S // L
    q = q.astype(jnp.float32) * (dqk ** -0.5)
    k = k.astype(jnp.float32)
    v = v.astype(jnp.float32)
    log_i = i_pre.astype(jnp.float32)
    log_f = jax.nn.log_sigmoid(f_pre.astype(jnp.float32))

    def to_chunks(t):
        return t.reshape(B, nc, L, H, t.shape[-1]).transpose(1, 0, 3, 2, 4)

    def gate_chunks(g):
        return g.reshape(B, nc, L, H).transpose(1, 0, 3, 2)

    mask = jnp.tril(jnp.ones((L, L), dtype=bool))

    def step(carry, xs):
        C, n, m = carry
        qc, kc, vc, li, lf = xs
        b = jnp.cumsum(lf, axis=-1)
        Dm = b[..., :, None] - b[..., None, :] + li[..., None, :]
        Dm = jnp.where(mask, Dm, -jnp.inf)
        inter = b + m[..., None]
        m_row = jnp.maximum(inter, jnp.max(Dm, axis=-1))
        w_intra = jnp.exp(Dm - m_row[..., None])
        s_inter = jnp.exp(inter - m_row)
        qk = jnp.einsum('bhld,bhsd->bhls', qc, kc) * w_intra
        num = s_inter[..., None] * jnp.einsum('bhld,bhde->bhle', qc, C) \
            + jnp.einsum('bhls,bhse->bhle', qk, vc)
        den = s_inter * jnp.einsum('bhld,bhd->bhl', qc, n) + jnp.sum(qk, axis=-1)
        h = num / jnp.maximum(jnp.abs(den), jnp.exp(-m_row))[..., None]
        b_last = b[..., -1]
        a = b_last[..., None] - b + li
        m_new = jnp.maximum(b_last + m, jnp.max(a, axis=-1))
        s_state = jnp.exp(b_last + m - m_new)
        kw = kc * jnp.exp(a - m_new[..., None])[..., None]
        C_new = s_state[..., None, None] * C + jnp.einsum('bhsd,bhse->bhde', kw, vc)
        n_new = s_state[..., None] * n + jnp.sum(kw, axis=-2)
        return (C_new, n_new, m_new), h

    init = (jnp.zeros((B, H, dqk, dv), jnp.float32),
            jnp.zeros((B, H, dqk), jnp.float32),
            jnp.zeros((B, H), jnp.float32))
    xs = (to_chunks(q), to_chunks(k), to_chunks(v), gate_chunks(log_i), gate_chunks(log_f))
    _, h = lax.scan(step, init, xs)
    h = h.transpose(1, 0, 3, 2, 4).reshape(B, S, H, dv)
    return h.astype(out_dtype)


def causal_depthwise_conv(u, w):
    S = u.shape[1]
    up = jnp.pad(u, ((0, 0), (CONV_K - 1, 0), (0, 0)))
    y = w[CONV_K - 1] * up[:, CONV_K - 1:CONV_K - 1 + S, :]
    for j in range(CONV_K - 1):
        y = y + w[j] * up[:, j:j + S, :]
    return y


def hybrid_mixer(h, w_in, b_if, mlstm_head_norm, conv_w, w_branch, w_mix_out):
    B, S, _ = h.shape
    proj = h @ w_in
    sizes = [MLSTM_QK, MLSTM_QK, MLSTM_V, MLSTM_V, MLSTM_HEADS, MLSTM_HEADS,
             CONV_WIDTH, CONV_WIDTH, CONV_WIDTH, N_BRANCH * D_MODEL]
    idx = [int(s) for s in np.cumsum(sizes)[:-1]]
    q, k, v, o_pre, i_pre, f_pre, cb, cc, cx, g_pre = jnp.split(proj, idx, axis=-1)

    q = q.reshape(B, S, MLSTM_HEADS, MLSTM_DQK)
    k = k.reshape(B, S, MLSTM_HEADS, MLSTM_DQK)
    v = v.reshape(B, S, MLSTM_HEADS, MLSTM_DV)
    i_pre = GATE_CAP * jnp.tanh((i_pre + b_if[0]) / GATE_CAP)
    f_pre = GATE_CAP * jnp.tanh((f_pre + b_if[1]) / GATE_CAP)
    hA = mlstm_chunkwise(q, k, v, i_pre, f_pre)
    hA = rms_norm(hA, mlstm_head_norm.reshape(MLSTM_HEADS, MLSTM_DV))
    hA = jax.nn.sigmoid(o_pre) * hA.reshape(B, S, MLSTM_V)

    hB = cb * causal_depthwise_conv(cc * cx, conv_w)

    ys = jnp.einsum('bsnc,ncd->bsnd', jnp.stack([hA, hB], axis=2), w_branch)
    gates = jax.nn.sigmoid(g_pre.reshape(B, S, N_BRANCH, D_MODEL))
    merged = jnp.sum(gates * ys, axis=2)
    return merged @ w_mix_out


def memory_cross_attention(h, m, w_xq, w_xk, w_xv, w_xo):
    B, S, _ = h.shape
    M = m.shape[1]
    q = (h @ w_xq).reshape(B, S, XATTN_HEADS, XATTN_DH)
    k = (m @ w_xk).reshape(B, M, XATTN_HEADS, XATTN_DH)
    v = (m @ w_xv).reshape(B, M, XATTN_HEADS, XATTN_DH)
    s = jnp.einsum('bshd,bmhd->bhsm', q.astype(jnp.float32), k.astype(jnp.float32))
    p = jax.nn.softmax(s * (XATTN_DH ** -0.5), axis=-1).astype(v.dtype)
    o = jnp.einsum('bhsm,bmhd->bshd', p, v).reshape(B, S, XATTN_HEADS * XATTN_DH)
    return o @ w_xo


def swiglu_ffn(h, w_gate, w_up, w_down):
    return (jax.nn.silu(h @ w_gate) * (h @ w_up)) @ w_down


def setup_inputs(seed: int = 0) -> dict:
    key = jax.random.key(seed)
    ks = jax.random.split(key, 24)

    def nrm(k, shape, scale):
        return jax.random.normal(k, shape, jnp.float32) * scale

    def gain(k, shape):
        return 1.0 + 0.02 * jax.random.normal(k, shape, jnp.float32)

    b_i = 0.1 * jax.random.normal(ks[4], (DEPTH, MLSTM_HEADS), jnp.float32)
    b_f = jnp.linspace(3.0, 6.0, MLSTM_HEADS, dtype=jnp.float32)[None, :] \
        + 0.1 * jax.random.normal(ks[5], (DEPTH, MLSTM_HEADS), jnp.float32)
    return {
        "x": nrm(ks[0], (BATCH, SEQ, D_MODEL), 1.0),
        "mem": nrm(ks[1], (BATCH, MEM_LEN, D_MODEL), 1.0),
        "norm_pre_mix": gain(ks[2], (DEPTH, D_MODEL)),
        "norm_post_mix": gain(ks[3], (DEPTH, D_MODEL)),
        "w_in": nrm(ks[6], (DEPTH, D_MODEL, IN_WIDTH), D_MODEL ** -0.5),
        "b_if": jnp.stack([b_i, b_f], axis=1),
        "mlstm_head_norm": gain(ks[7], (DEPTH, MLSTM_V)),
        "conv_w": nrm(ks[8], (DEPTH, CONV_K, CONV_WIDTH), CONV_K ** -0.5),
        "w_branch": nrm(ks[9], (DEPTH, N_BRANCH, MLSTM_V, D_MODEL), MLSTM_V ** -0.5),
        "w_mix_out": nrm(ks[10], (DEPTH, D_MODEL, D_MODEL), D_MODEL ** -0.5),
        "norm_pre_xattn": gain(ks[11], (DEPTH, D_MODEL)),
        "norm_post_xattn": gain(ks[12], (DEPTH, D_MODEL)),
        "norm_mem": gain(ks[13], (DEPTH, D_MODEL)),
        "w_xq": nrm(ks[14], (DEPTH, D_MODEL, XATTN_HEADS * XATTN_DH), D_MODEL ** -0.5),
        "w_xk": nrm(ks[15], (DEPTH, D_MODEL, XATTN_HEADS * XATTN_DH), D_MODEL ** -0.5),
        "w_xv": nrm(ks[16], (DEPTH, D_MODEL, XATTN_HEADS * XATTN_DH), D_MODEL ** -0.5),
        "w_xo": nrm(ks[17], (DEPTH, XATTN_HEADS * XATTN_DH, D_MODEL), (XATTN_HEADS * XATTN_DH) ** -0.5),
        "norm_pre_ffn": gain(ks[18], (DEPTH, D_MODEL)),
        "norm_post_ffn": gain(ks[19], (DEPTH, D_MODEL)),
        "w_ffn_gate": nrm(ks[20], (DEPTH, D_MODEL, D_FF), D_MODEL ** -0.5),
        "w_ffn_up": nrm(ks[21], (DEPTH, D_MODEL, D_FF), D_MODEL ** -0.5),
        "w_ffn_down": nrm(ks[22], (DEPTH, D_FF, D_MODEL), D_FF ** -0.5),
    }


def reference(x, mem, norm_pre_mix, norm_post_mix, w_in, b_if, mlstm_head_norm, conv_w,
              w_branch, w_mix_out, norm_pre_xattn, norm_post_xattn, norm_mem,
              w_xq, w_xk, w_xv, w_xo, norm_pre_ffn, norm_post_ffn,
              w_ffn_gate, w_ffn_up, w_ffn_down):
    for l in range(DEPTH):
        h = rms_norm(x, norm_pre_mix[l])
        y = hybrid_mixer(h, w_in[l], b_if[l], mlstm_head_norm[l], conv_w[l],
                         w_branch[l], w_mix_out[l])
        x = x + rms_norm(y, norm_post_mix[l])
        h = rms_norm(x, norm_pre_xattn[l])
        m = rms_norm(mem, norm_mem[l])
        y = memory_cross_attention(h, m, w_xq[l], w_xk[l], w_xv[l], w_xo[l])
        x = x + rms_norm(y, norm_post_xattn[l])
        h = rms_norm(x, norm_pre_ffn[l])
        y = swiglu_ffn(h, w_ffn_gate[l], w_ffn_up[l], w_ffn_down[l])
        x = x + rms_norm(y, norm_post_ffn[l])
    return x
```

```python
import contextlib
import numpy as np
import concourse.bass as bass
import concourse.mybir as mybir
from concourse.bass_utils import run_bass_kernel_spmd

F32 = mybir.dt.float32
BF16 = mybir.dt.bfloat16
AF = mybir.ActivationFunctionType
ALU = mybir.AluOpType
AX = mybir.AxisListType

D = 4096
DFF = 11008
KC = D // 128
FC = DFF // 128
C_Q, C_K, C_V, C_O, C_IF, C_CB, C_CC, C_CX, C_G = 0, 1024, 2048, 4096, 6144, 6152, 8200, 10248, 12296
EPS = 1e-6
NW = 3


class Ins:
    __slots__ = ("eng", "meth", "args", "kw", "deps", "done", "marked", "key")

    def __init__(self, eng, meth, args, kw, deps, key=None):
        self.eng, self.meth, self.args, self.kw = eng, meth, args, kw
        self.deps = deps
        self.done = None
        self.marked = False
        self.key = key
        for d in deps:
            d.marked = True


class Buf:
    __slots__ = ("w", "r", "name", "excl")

    def __init__(self, name="", excl=False):
        self.w = {}
        self.r = {}
        self.name = name
        self.excl = excl


class Prog:
    ENG = ("pe", "act", "dve", "pool", "sp")

    def __init__(self, dry):
        self.dry = dry
        self.q = {e: [] for e in self.ENG}
        self.dma_cnt = {}
        self.fence = []
        self.last = {}
        self.last_dma = {}
        self.dummy = Ins("x", "x", (), {}, [])

    def _mk(self, eng, meth, args, kw, deps, key=None):
        dd = []
        seen = set()
        for d in deps:
            if d is None or d is self.dummy:
                continue
            if d.key is None and d.eng == eng and eng == "pe":
                continue
            if id(d) in seen:
                continue
            seen.add(id(d))
            dd.append(d)
        i = Ins(eng, meth, args, kw, dd, key=key)
        self.q[eng].append(i)
        return i

    def op(self, eng, meth, *args, R=(), W=(), deps=(), **kw):
        if self.dry:
            return self.dummy
        d = list(deps)
        for f in self.fence:
            if f.key is not None or f.eng != eng:
                d.append(f)
        for b in R:
            d.extend(b.w.values())
            if b.excl:
                for k, v in b.r.items():
                    if k != eng:
                        d.append(v)
        for b in W:
            for k, v in b.w.items():
                if k != eng:
                    d.append(v)
            for k, v in b.r.items():
                if k != eng:
                    d.append(v)
        i = self._mk(eng, meth, args, kw, d)
        for b in R:
            b.r[eng] = i
        for b in W:
            b.r = {}
            b.w[eng] = i
        self.last[eng] = i
        return i

    def dma(self, eng, out, in_, key, R=(), W=(), deps=(), nofence=False):
        if self.dry:
            return self.dummy
        d = list(deps) + ([] if nofence else list(self.fence))
        for b in R:
            d.extend(b.w.values())
        for b in W:
            d.extend(b.w.values())
            d.extend(b.r.values())
        i = self._mk(eng, "dma_start", (), dict(out=out, in_=in_), d, key=key)
        self.dma_cnt[key] = self.dma_cnt.get(key, 0) + 1
        i.done = (("dma", key), 16 * self.dma_cnt[key])
        for b in R:
            b.r[key] = i
        for b in W:
            b.r = {}
            b.w[key] = i
        if eng == "sp" and not nofence:
            self.last_dma[key] = i
        return i

    def barrier(self):
        if self.dry:
            return
        self.fence = [self.last[e] for e in ("pe", "act", "dve") if e in self.last] + list(self.last_dma.values())

    def emit(self, nc, st, final_waits):
        for e in self.ENG:
            c = 0
            for i in self.q[e]:
                if i.key is None and i.marked:
                    c += 1
                    i.done = (("eng", e), c)
        sems = {}
        for e in ("pe", "act", "dve", "pool"):
            sems[("eng", e)] = st.enter_context(nc.semaphore("s_" + e))
        for k in self.dma_cnt:
            sems[("dma", k)] = st.enter_context(nc.semaphore("d_" + str(k)))
        block = st.enter_context(nc.Block())
        prog = self

        def run(engname, eng):
            waited = {}
            for i in prog.q[engname]:
                for d in i.deps:
                    s, v = d.done
                    if waited.get(s, 0) < v:
                        eng.wait_ge(sems[s], v)
                        waited[s] = v
                bi = getattr(eng, i.meth)(*i.args, **i.kw)
                if i.key is not None:
                    bi.then_inc(sems[("dma", i.key)], 16)
                elif i.marked:
                    bi.then_inc(sems[i.done[0]], 1)
            for d in final_waits.get(engname, ()):
                s, v = d.done
                if waited.get(s, 0) < v:
                    eng.wait_ge(sems[s], v)
                    waited[s] = v

        @block.tensor
        def _(eng):
            run("pe", eng)

        @block.scalar
        def _(eng):
            run("act", eng)

        @block.vector
        def _(eng):
            run("dve", eng)

        @block.gpsimd
        def _(eng):
            run("pool", eng)

        @block.sync
        def _(eng):
            run("sp", eng)


class _Stop(Exception):
    pass


def build(NLOC, NPRE, NT, debug=False, stop=None):
    TT = 128 * NT
    assert NLOC % NT == 0 and NPRE % NT == 0 and NT == 2
    nc = bass.Bass("TRN2", target_bir_lowering=False)

    def din(name, shape):
        return nc.dram_tensor(name, shape, F32, kind="ExternalInput").ap()

    xl = din("xl", [NLOC * 128, D])
    xp = din("xp", [NPRE * 128, D])
    memb = din("memb", [256, D])
    w_in = din("w_in", [D, 20488])
    w_br = din("w_br", [D, D])
    w_mo = din("w_mo", [D, D])
    w_xq = din("w_xq", [D, D])
    w_xk = din("w_xk", [D, D])
    w_xv = din("w_xv", [D, D])
    w_xo = din("w_xo", [D, D])
    w_fg = din("w_fg", [D, DFF])
    w_fu = din("w_fu", [D, DFF])
    w_fd = din("w_fd", [DFF, D])
    nw_bc = din("nw_bc", [7, 128, D])
    hnw_bc = din("hnw_bc", [128, 2048])
    bif_bc = din("bif_bc", [128, 8])
    convw_c = din("convw_c", [128, 48])
    out = nc.dram_tensor("out", [NLOC * 128, D], F32, kind="ExternalOutput").ap()
    dbg = {}
    if debug:
        for nm_ in ("dbg_x1", "dbg_x2"):
            dbg[nm_] = nc.dram_tensor(nm_, [NLOC * 128, D], F32, kind="ExternalOutput").ap()

    class WV:
        def __init__(self, ap, name):
            self.ap, self.name = ap, name

    def wview(w, name):
        return WV(w.rearrange("(kc p) n -> p kc n", p=128), name)

    st = contextlib.ExitStack()
    with st:
        def sb(name, shape, dt):
            return st.enter_context(nc.sbuf_tensor(name, shape, dt))

        ident = sb("ident", [128, 128], BF16)
        identf = sb("identf", [128, 128], F32)
        tri = sb("tri", [128, 128], F32)
        nmask = sb("nmask", [128, 128], F32)
        ones = sb("ones", [128, 128], F32)
        zeros = identf
        onesb = sb("onesb", [128, 1], BF16)
        epsc = sb("epsc", [128, 1], F32)
        bif = sb("bif", [128, 8], F32)
        convw = sb("convw", [128, 48], F32)
        wgate = sb("wgate", [128, KC, 8], BF16)
        cols = sb("cols", [128, 256 + 16], F32)
        gcols = sb("gcols", [128, 2, 96], F32)
        junk = sb("junk", [128, 512], BF16)
        wbc = sb("wbc", [128, D], F32)
        Cst = sb("Cst", [128, 8, 512], F32)
        Cbf = sb("Cbf", [128, 8, 512], BF16)
        nst = sb("nst", [128, 8], F32)
        nbf = sb("nbf", [128, 8], BF16)
        kmT = sb("kmT", [128, KC, 256], BF16)
        vm = sb("vm", [128, 2, D], BF16)
        hT = sb("hT", [128, KC, TT + 2], BF16)
        wslots = [sb(f"wslot{i}", [128, 8, 512], BF16) for i in range(NW)]
        arA = sb("arA", [128, NT * D], F32)
        arB = sb("arB", [128, NT * D], F32)
        arC = sb("arC", [128, 5120], F32)
        psF = [st.enter_context(nc.psum_tensor(f"psF{i}", [128, 512], F32)) for i in range(6)]
        psT = [st.enter_context(nc.psum_tensor(f"psT{i}", [128, 1024], BF16)) for i in range(2)]

        def bfv(ar, w0, w1, inner=None):
            v = ar[:, w0:w1].bitcast(BF16)
            if inner is not None:
                v = v.rearrange("p (c n) -> p c n", n=inner)
            return v

        def f3(ar, w0, w1, inner):
            return ar[:, w0:w1].rearrange("p (c n) -> p c n", n=inner)

        Rv = f3(arA, 0, NT * D, D)
        Yv = f3(arB, 0, NT * D, D)
        k_tok = bfv(arB, 0, NT * 512, 1024)
        v_tok = bfv(arB, NT * 512, NT * 1536, 2048)
        o0 = NT * 1536
        qT = bfv(arB, o0, o0 + 4 * TT, TT)
        kT = bfv(arB, o0 + 4 * TT, o0 + 8 * TT, TT)
        o1 = o0 + 8 * TT
        gs = f3(arB, o1, o1 + 4 * TT, TT)
        mg = f3(arB, o1 + 4 * TT, o1 + 8 * TT, TT)
        assert o1 + 8 * TT <= NT * D
        p_f = arB[:, 0:256]
        pn_b = bfv(arB, 256, 384)
        pT = bfv(arB, 384, 384 + TT, TT)
        sig_o = f3(arA, 0, NT * 2048, 2048)
        a0 = NT * 2048
        lfrep = f3(arA, a0, a0 + 512, 128)
        Ebc = arA[:, a0 + 512:a0 + 1024]
        Dm = [arA[:, a0 + 1024 + i * 128:a0 + 1152 + i * 128] for i in range(2)]
        Wt = [arA[:, a0 + 1280 + i * 128:a0 + 1408 + i * 128] for i in range(2)]
        PTt = [bfv(arA, a0 + 1536 + i * 64, a0 + 1600 + i * 64) for i in range(2)]
        qp = [bfv(arA, a0 + 1664 + i * 128, a0 + 1792 + i * 128, 128) for i in range(2)]
        hraw = [arA[:, a0 + 1920 + i * 512:a0 + 2432 + i * 512] for i in range(2)]
        tq = arA[:, a0 + 2944:a0 + 3456]
        kw = bfv(arA, a0 + 3456, a0 + 3968)
        assert a0 + 3968 <= NT * D
        ccsb = f3(arA, 0, 4 * (TT + 2), TT + 2)
        ubuf = f3(arA, 4 * (TT + 2), 8 * (TT + 2), TT + 2)
        cacc = f3(arA, 8 * (TT + 2), 8 * (TT + 2) + 4 * TT, TT)
        assert 8 * (TT + 2) + 4 * TT <= a0
        mergedT = bfv(arA, a0, a0 + 16 * TT, TT)
        assert a0 + 16 * TT <= NT * D
        xn2 = [bfv(arC, 0, 2048), bfv(arC, 2048, 4096)]
        hBT = bfv(arC, 0, 8 * TT, TT)
        hAT = bfv(arC, 2048, 2048 + 8 * TT, TT)
        hA_tok = bfv(arC, 4096, 5120)
        oT = bfv(arC, 0, 16 * TT, TT)
        q2T = bfv(arC, 4096, 4096 + 4 * TT, TT)
        aTp = [bfv(arC, i * 4 * TT, (i + 1) * 4 * TT, TT) for i in range(2)]
        sgt = [f3(arC, 2048 + i * 4 * TT, 2048 + (i + 1) * 4 * TT, TT) for i in range(2)]
        assert 2048 + 8 * TT <= 5120 and 16 * TT <= 4096

        win_v = wview(w_in, "in")

        def schedule(P, plan, nplanned):
            try:
                return schedule_(P, plan, nplanned)
            except _Stop:
                P.barrier()
                od = P.dma("sp", out[0:128, :], arB[:, 0:D], "out")
                if debug:
                    P.dma("sp", dbg["dbg_x1"][0:128, :], arA[:, 0:D], "dbg")
                return [od]

        def chk(name):
            if stop == name:
                raise _Stop()

        def schedule_(P, plan, nplanned):
            B = {}

            def bf(name):
                if name not in B:
                    B[name] = Buf(name)
                return B[name]

            bankF = [Buf(f"psF{i}", True) for i in range(6)]
            bankT = [Buf(f"psT{i}", True) for i in range(2)]
            rr = {"f": 0, "t": 0, "e": 0}

            def psget():
                b = rr["f"] % 5
                rr["f"] += 1
                return psF[b], bankF[b]

            def ptget():
                b = rr["t"] % 2
                rr["t"] += 1
                return psT[b], bankT[b]

            def evac_eng():
                rr["e"] += 1
                return "act" if rr["e"] % 2 else "dve"

            def copy(eng, out_, in_, R, W, scale=None):
                if eng == "act":
                    if scale is None:
                        return P.op("act", "activation", out_, in_, AF.Copy, R=R, W=W)
                    return P.op("act", "activation", out_, in_, AF.Copy, scale=scale, R=R, W=W)
                if scale is None:
                    return P.op("dve", "tensor_copy", out_, in_, R=R, W=W)
                return P.op("dve", "tensor_scalar", out_, in_, scale, None, ALU.mult, R=R, W=W)

            wsb = [Buf(f"ws{i}") for i in range(NW)]
            wstate = {"next_get": 0, "next_load": 0}

            wb_ins = {}
            phase = {"prefix": False}
            stg = [bfv(arA, 0, 2048, 512), bfv(arA, 2048, 4096, 512), bfv(arA, 4608, 6656, 512)]
            stgb = [Buf(f"stg{j}") for j in range(3)]
            conv_last = {}

            def preconvert_burst():
                if P.dry:
                    return
                wbq = []
                for n_, (src, nk, ncols, key) in enumerate(wcache["pre"]):
                    j = n_ % 3
                    P.dma("pool", stg[j][:, 0:nk, 0:ncols], src, f"cv{j}", W=[stgb[j]], nofence=True)
                    wbq.append((j, key))
                    if len(wbq) == 3:
                        flush_one(wbq)
                while wbq:
                    flush_one(wbq)

            def flush_one(wbq):
                j, key = wbq.pop(0)
                wb = P.dma("pool", wcache["t"][wcache["idx"][key]], stg[j][:].rearrange("p k n -> p (k n)"), f"cwb{j}",
                           R=[stgb[j]], nofence=True)
                wb_ins[key] = wb
                conv_last[f"cwb{j}"] = wb

            def wload(m):
                src, nk, ncols, key = plan[m]
                s_ = m % NW
                flat = wslots[s_][:].rearrange("p k n -> p (k n)")
                if key in wb_ins:
                    P.dma("sp", flat, wcache["t"][wcache["idx"][key]], f"w{s_}", W=[wsb[s_]], deps=[wb_ins[key]], nofence=True)
                    return
                P.dma("pool", wslots[s_][:, 0:nk, 0:ncols], src, f"w{s_}", W=[wsb[s_]], nofence=True)
                if key in wcache["idx"]:
                    wb_ins[key] = P.dma("sp", wcache["t"][wcache["idx"][key]], flat, f"wb{s_}", R=[wsb[s_]], nofence=True)

            def wget(src, nk, ncols, key):
                if P.dry:
                    plan.append((src, nk, ncols, key))
                    return wslots[0], wsb[0]
                n = wstate["next_get"]
                wstate["next_get"] += 1
                while wstate["next_load"] < min(n + NW, nplanned):
                    wload(wstate["next_load"])
                    wstate["next_load"] += 1
                return wslots[n % NW], wsb[n % NW]

            def formA(Wv, row0, K, col0, ncols, lhs_view, lhs_bufs, ntl, evac):
                banks = [psget() for _ in range(ntl)]
                for k0 in range(0, K, 8):
                    nk = min(8, K - k0)
                    wt, wb = wget(Wv.ap[:, row0 + k0:row0 + k0 + nk, col0:col0 + ncols], nk, ncols, (Wv.name, row0 + k0, col0, nk, ncols))
                    for kl in range(nk):
                        kc = k0 + kl
                        for i in range(ntl):
                            P.op("pe", "matmul", banks[i][0][:, 0:ncols], lhs_view(kc, i), wt[:, kl, 0:ncols],
                                 start=(kc == 0), stop=(kc == K - 1), R=[wb] + lhs_bufs, W=[banks[i][1]])
                for i in range(ntl):
                    evac(i, banks[i][0][:, 0:ncols], banks[i][1])

            def formW(Wv, row0, K, col0, ncols, rhs_view, rhs_bufs, N, evac):
                nm = ncols // 128
                banks = [psget() for _ in range(nm)]
                for k0 in range(0, K, 8):
                    nk = min(8, K - k0)
                    wt, wb = wget(Wv.ap[:, row0 + k0:row0 + k0 + nk, col0:col0 + ncols], nk, ncols, (Wv.name, row0 + k0, col0, nk, ncols))
                    for kl in range(nk):
                        kc = k0 + kl
                        for m in range(nm):
                            P.op("pe", "matmul", banks[m][0][:, 0:N], wt[:, kl, m * 128:(m + 1) * 128], rhs_view(kc),
                                 start=(kc == 0), stop=(kc == K - 1), R=[wb] + rhs_bufs, W=[banks[m][1]])
                for m in range(nm):
                    evac(m, banks[m][0][:, 0:N], banks[m][1])

            hTB = bf("hT")

            def hcols(i):
                return slice(2 + i * 128, 2 + (i + 1) * 128)

            P.op("pool", "memset", ones[:], 1.0, W=[bf("ones")])
            P.op("pool", "memset", onesb[:], 1.0, W=[bf("onesb")])
            P.op("pool", "memset", epsc[:], EPS, W=[bf("epsc")])
            P.op("pool", "memset", zeros[:], 0.0, W=[bf("identf")])
            P.op("pool", "affine_select", tri[:], ones[:], pattern=[[1, 128]], compare_op=ALU.is_ge, fill=0.0,
                 base=0, channel_multiplier=-1, R=[bf("ones")], W=[bf("tri")])
            P.op("pool", "affine_select", nmask[:], zeros[:], pattern=[[1, 128]], compare_op=ALU.is_ge, fill=-1e30,
                 base=0, channel_multiplier=-1, R=[bf("identf")], W=[bf("nmask")])
            P.op("pool", "affine_select", identf[:], ones[:], pattern=[[-1, 128]], compare_op=ALU.is_equal, fill=0.0,
                 base=0, channel_multiplier=1, R=[bf("ones"), bf("nmask")], W=[bf("identf")])
            P.op("dve", "tensor_copy", ident[:], identf[:], R=[bf("identf")], W=[bf("ident")])
            P.op("dve", "memset", Cst[:], 0.0, W=[bf("Cst")])
            P.op("dve", "memset", Cbf[:], 0.0, W=[bf("Cbf")])
            P.op("dve", "memset", nst[:], 0.0, W=[bf("nst")])
            P.op("dve", "memset", nbf[:], 0.0, W=[bf("nbf")])
            P.op("dve", "memset", hT[:], 0.0, W=[hTB])
            P.dma("sp", bif[:], bif_bc, "small0", W=[bf("bif")])
            P.dma("sp", convw[:], convw_c, "small1", W=[bf("convw")])
            P.dma("pool", wgate[:], win_v.ap[:, :, C_IF:C_IF + 8], "wgate", W=[bf("wgate")])
            cidx = {"n": 0}

            def col(n=1):
                sl = cidx["n"] % 32
                cidx["n"] += 1
                return cols[:, sl * 8:sl * 8 + n], bf(f"cs{sl}")

            gidx = {"n": 0, "par": 0}

            def gcol(n):
                sl = gidx["n"]
                gidx["n"] += 1
                assert sl < 12
                return gcols[:, gidx["par"], sl * 8:sl * 8 + n], bf(f"gc{gidx['par']}_{sl}")

            def load_wbc(idx):
                P.dma("sp", wbc[:], nw_bc[idx], "wbc", W=[bf("wbc")])

            def rstd_from(ss_ap, ss_b, scale):
                r_ap, r_b = col()
                P.op("act", "activation", r_ap, ss_ap, AF.Sqrt, bias=epsc[:, 0:1], scale=scale, R=[ss_b, bf("epsc")], W=[r_b])
                r2, r2b = col()
                P.op("dve", "reciprocal", r2, r_ap, R=[r_b], W=[r2b])
                return r2, r2b

            def norm_T(src_view, src_buf, ntiles):
                for i in range(ntiles):
                    xn, xnb = xn2[i % 2], bf(f"xn{i % 2}")
                    ss, ssb = col()
                    P.op("act", "activation", xn[:, :], src_view(i), AF.Square, accum_out=ss, R=[src_buf(i)], W=[xnb, ssb])
                    rs, rsb = rstd_from(ss, ssb, 1.0 / D)
                    P.op("dve", "scalar_tensor_tensor", xn[:, :], src_view(i), rs, wbc[:], ALU.mult, ALU.mult,
                         R=[src_buf(i), rsb, bf("wbc")], W=[xnb])
                    for g in range(4):
                        pt, ptb = ptget()
                        for c in range(8):
                            kc = g * 8 + c
                            P.op("pe", "transpose", pt[:, c * 128:(c + 1) * 128], xn[:, kc * 128:(kc + 1) * 128], ident[:],
                                 R=[xnb, bf("ident")], W=[ptb])
                        copy(evac_eng(), hT[:, g * 8:(g + 1) * 8, hcols(i)], pt[:, :].rearrange("p (c n) -> p c n", n=128),
                             R=[ptb], W=[hTB])

            def lhsH(kc, i):
                return hT[:, kc, hcols(i)]

            def gates(i, local):
                gidx["par"] ^= 1
                gidx["n"] = 0
                col_ = gcol
                ps, pb = psget()
                for kc in range(KC):
                    P.op("pe", "matmul", ps[:, 0:8], lhsH(kc, i), wgate[:, kc, :], start=(kc == 0), stop=(kc == KC - 1),
                         R=[hTB, bf("wgate")], W=[pb])
                g, gb = col_(8)
                P.op("dve", "tensor_tensor", g, ps[:, 0:8], bif[:], ALU.add, R=[pb, bf("bif")], W=[gb])
                t, tb = col_(8)
                P.op("act", "activation", t, g, AF.Tanh, scale=1.0 / 15.0, R=[gb], W=[tb])
                li, lib = col_(4)
                P.op("dve", "tensor_scalar", li, t[:, 0:4], 15.0, None, ALU.mult, R=[tb], W=[lib])
                e, eb_ = col_(4)
                P.op("act", "activation", e, t[:, 4:8], AF.Exp, scale=-15.0, R=[tb], W=[eb_])
                sp_, spb = col_(4)
                P.op("act", "activation", sp_, e, AF.Ln, bias=1.0, R=[eb_], W=[spb])
                lf, lfb = col_(4)
                P.op("dve", "tensor_scalar", lf, sp_, -1.0, None, ALU.mult, R=[spb], W=[lfb])
                for h in range(4):
                    P.op("dve", "tensor_scalar", lfrep[:, h, :], ones[:], lf[:, h:h + 1], None, ALU.mult,
                         R=[lfb, bf("ones")], W=[bf("lfrep")])
                ps2, pb2 = psget()
                P.op("pe", "matmul", ps2[:, 0:4], tri[:], lf, start=True, stop=True, R=[bf("tri"), lfb], W=[pb2])
                bcol, bcb = col_(4)
                P.op("dve", "tensor_copy", bcol, ps2[:, 0:4], R=[pb2], W=[bcb])
                ps3, pb3 = psF[5], bankF[5]
                for h in range(4):
                    P.op("pe", "matmul", ps3[:, h * 128:(h + 1) * 128], lfrep[:, h, :], tri[:], start=True, stop=True,
                         R=[bf("lfrep"), bf("tri")], W=[pb3])
                btot = ps3[:, :].rearrange("p (h l) -> p h l", l=128)[:, :, 127]
                a, ab = col_(4)
                P.op("dve", "tensor_tensor", a, btot, bcol, ALU.subtract, R=[pb3, bcb], W=[ab])
                a2, a2b = col_(4)
                P.op("dve", "tensor_tensor", a2, a, li, ALU.add, R=[ab, lib], W=[a2b])
                aw, awb = col_(4)
                P.op("act", "activation", aw, a2, AF.Exp, R=[a2b], W=[awb])
                ebt, ebb = col_(4)
                P.op("act", "activation", ebt, btot, AF.Exp, R=[pb3], W=[ebb])
                if local:
                    P.op("act", "activation", Ebc, ps3[:, :], AF.Exp, R=[pb3], W=[bf("Ebc")])
                return dict(li=li, lib=lib, bcol=bcol, bcb=bcb, ps3=ps3, pb3=pb3, aw=aw, awb=awb, eb=ebt, ebb=ebb)

            def state_update(i, G):
                ktb, vtb = bf("k_tok"), bf("v_tok")
                for h in range(4):
                    P.op("dve", "tensor_scalar", kw[:, h * 256:(h + 1) * 256], k_tok[:, i, h * 256:(h + 1) * 256],
                         G["aw"][:, h:h + 1], None, ALU.mult, R=[ktb, G["awb"]], W=[bf("kw")])
                eb8, eb8b = col(8)
                e8v = eb8.rearrange("p (h j) -> p h j", j=2)
                for j in range(2):
                    P.op("dve", "tensor_copy", e8v[:, :, j], G["eb"], R=[G["ebb"]], W=[eb8b])
                for c in range(8):
                    h = c // 2
                    ps, pb = psget()
                    P.op("pe", "matmul", ps[:, :], kw[:, c * 128:(c + 1) * 128], v_tok[:, i, h * 512:(h + 1) * 512],
                         start=True, stop=True, R=[bf("kw"), vtb], W=[pb])
                    P.op("dve", "scalar_tensor_tensor", Cst[:, c, :], Cst[:, c, :], G["eb"][:, h:h + 1], ps[:, :],
                         ALU.mult, ALU.add, R=[pb, G["ebb"], bf("Cst")], W=[bf("Cst")])
                    P.op("act", "activation", Cbf[:, c, :], Cst[:, c, :], AF.Copy, R=[bf("Cst")], W=[bf("Cbf")])
                psn, pnb = psget()
                for c in range(8):
                    P.op("pe", "matmul", psn[:, c:c + 1], kw[:, c * 128:(c + 1) * 128], onesb[:], start=True, stop=True,
                         R=[bf("kw"), bf("onesb")], W=[pnb])
                t8, t8b = col(8)
                P.op("dve", "tensor_tensor", t8, nst[:], eb8, ALU.mult, R=[bf("nst"), eb8b], W=[t8b])
                P.op("dve", "tensor_tensor", nst[:], t8, psn[:, 0:8], ALU.add, R=[t8b, pnb], W=[bf("nst")])
                P.op("dve", "tensor_copy", nbf[:], nst[:], R=[bf("nst")], W=[bf("nbf")])

            def evac_tok(dst3, gbase):
                def f(i, ps, pb):
                    copy(evac_eng(), dst3[:, i, gbase:gbase + 512], ps, R=[pb], W=[bf("tokdst")])
                return f

            def proj_kv(ntl):
                ktb, vtb = bf("k_tok"), bf("v_tok")

                def ek(g):
                    def f(i, ps, pb):
                        copy(evac_eng(), k_tok[:, i, g * 512:(g + 1) * 512], ps, R=[pb], W=[ktb])
                    return f

                def ev(g):
                    def f(i, ps, pb):
                        copy(evac_eng(), v_tok[:, i, g * 512:(g + 1) * 512], ps, R=[pb], W=[vtb])
                    return f
                for g in range(2):
                    formA(win_v, 0, KC, C_K + g * 512, 512, lhsH, [hTB], ntl, ek(g))
                for g in range(4):
                    formA(win_v, 0, KC, C_V + g * 512, 512, lhsH, [hTB], ntl, ev(g))

            def halo_copy():
                P.op("dve", "tensor_copy", hT[:, :, 0:2], hT[:, :, TT:TT + 2], R=[hTB], W=[hTB])

            chk("setup")
            load_wbc(4)
            for i in range(2):
                P.dma("sp", Yv[:, i, :], memb[i * 128:(i + 1) * 128, :], f"xin{i}", W=[bf(f"Y{i}")])
            norm_T(lambda i: Yv[:, i, :], lambda i: bf(f"Y{i}"), 2)
            xk_v, xv_v = wview(w_xk, "xk"), wview(w_xv, "xv")
            for g in range(8):
                def ekm(m, ps, pb, g=g):
                    copy(evac_eng(), kmT[:, g * 4 + m, :], ps, R=[pb], W=[bf("kmT")], scale=1.0 / 32.0)
                formW(xk_v, 0, KC, g * 512, 512, lambda kc: hT[:, kc, 2:258], [hTB], 256, ekm)
            for g in range(8):
                def evm(i, ps, pb, g=g):
                    copy(evac_eng(), vm[:, i, g * 512:(g + 1) * 512], ps, R=[pb], W=[bf("vm")])
                formA(xv_v, 0, KC, g * 512, 512, lhsH, [hTB], 2, evm)
            P.barrier()
            P.op("dve", "memset", hT[:, :, 0:2], 0.0, W=[hTB])
            chk("mem")

            phase["prefix"] = True
            for pg in range(NPRE // NT):
                P.barrier()
                if pg == 0:
                    load_wbc(0)
                for i in range(NT):
                    r0 = (pg * NT + i) * 128
                    P.dma("sp", Yv[:, i, :], xp[r0:r0 + 128, :], f"xin{i}", W=[bf(f"Y{i}")])
                norm_T(lambda i: Yv[:, i, :], lambda i: bf(f"Y{i}"), NT)
                P.barrier()
                proj_kv(NT)
                if pg == 0:
                    preconvert_burst()
                for i in range(NT):
                    G = gates(i, False)
                    state_update(i, G)
                halo_copy()

            phase["prefix"] = False
            if P.dry:
                wcache["local0"] = len(plan)
            else:
                P.last_dma.update(conv_last)

            chk("prefix")
            br_v, mo_v, xq_v, xo_v = wview(w_br, "br"), wview(w_mo, "mo"), wview(w_xq, "xq"), wview(w_xo, "xo")
            fg_v, fu_v, fd_v = wview(w_fg, "fg"), wview(w_fu, "fu"), wview(w_fd, "fd")
            out_dmas = []
            for ps_i in range(NLOC // NT):
                tok0 = ps_i * TT
                P.barrier()
                load_wbc(0)
                for i in range(NT):
                    P.dma("sp", Yv[:, i, :], xl[tok0 + i * 128:tok0 + (i + 1) * 128, :], f"xin{i}", W=[bf(f"Y{i}")])
                norm_T(lambda i: Yv[:, i, :], lambda i: bf(f"Y{i}"), NT)
                P.barrier()
                P.dma("sp", wbc[:, 0:2048], hnw_bc, "wbc", W=[bf("wbc")])
                proj_kv(NT)
                for g in range(4):
                    def eo(i, ps, pb, g=g):
                        P.op("act", "activation", sig_o[:, i, g * 512:(g + 1) * 512], ps, AF.Sigmoid, R=[pb], W=[bf("sig_o")])
                    formA(win_v, 0, KC, C_O + g * 512, 512, lhsH, [hTB], NT, eo)
                for g in range(2):
                    def eq(m, ps, pb, g=g):
                        copy(evac_eng(), qT[:, g * 4 + m, :], ps, R=[pb], W=[bf("qT")], scale=1.0 / 16.0)
                    formW(win_v, 0, KC, C_Q + g * 512, 512, lambda kc: hT[:, kc, 2:TT + 2], [hTB], TT, eq)
                chk("proj")
                for i in range(NT):
                    pt, ptb = ptget()
                    for c in range(8):
                        P.op("pe", "transpose", pt[:, c * 128:(c + 1) * 128], k_tok[:, i, c * 128:(c + 1) * 128], ident[:],
                             R=[bf("k_tok"), bf("ident")], W=[ptb])
                    copy(evac_eng(), kT[:, :, i * 128:(i + 1) * 128], pt[:, :].rearrange("p (c n) -> p c n", n=128),
                         R=[ptb], W=[bf("kT")])
                for i in range(NT):
                    G = gates(i, True)
                    tc_ = slice(i * 128, (i + 1) * 128)
                    for h in range(4):
                        par = h % 2
                        pst, pstb = psget()
                        for j in range(2):
                            P.op("pe", "matmul", pst[:, 0:128], kT[:, 2 * h + j, tc_], qT[:, 2 * h + j, tc_],
                                 start=(j == 0), stop=(j == 1), R=[bf("kT"), bf("qT")], W=[pstb])
                        P.op("dve", "scalar_tensor_tensor", Dm[par], G["ps3"][:, h * 128:(h + 1) * 128], G["bcol"][:, h:h + 1],
                             nmask[:], ALU.subtract, ALU.min, R=[G["pb3"], G["bcb"], bf("nmask")], W=[bf(f"Dm{par}")])
                        P.op("act", "activation", Wt[par], Dm[par], AF.Exp, bias=G["li"][:, h:h + 1],
                             R=[bf(f"Dm{par}"), G["lib"]], W=[bf(f"Wt{par}")])
                        P.op("dve", "tensor_tensor", PTt[par], pst[:, 0:128], Wt[par], ALU.mult,
                             R=[pstb, bf(f"Wt{par}")], W=[bf(f"PT{par}")])
                        for j in range(2):
                            P.op("dve", "tensor_tensor", qp[par][:, j, :], qT[:, 2 * h + j, tc_], Ebc[:, h * 128:(h + 1) * 128],
                                 ALU.mult, R=[bf("qT"), bf("Ebc")], W=[bf(f"qp{par}")])
                        pnum, pnumb = psget()
                        for j in range(2):
                            P.op("pe", "matmul", pnum[:, :], qp[par][:, j, :], Cbf[:, 2 * h + j, :], start=(j == 0), stop=False,
                                 R=[bf(f"qp{par}"), bf("Cbf")], W=[pnumb])
                        P.op("pe", "matmul", pnum[:, :], PTt[par], v_tok[:, i, h * 512:(h + 1) * 512], start=False, stop=True,
                             R=[bf(f"PT{par}"), bf("v_tok")], W=[pnumb])
                        pden, pdenb = psget()
                        for j in range(2):
                            P.op("pe", "matmul", pden[:, 0:1], qp[par][:, j, :], nbf[:, 2 * h + j:2 * h + j + 1], start=(j == 0),
                                 stop=False, R=[bf(f"qp{par}"), bf("nbf")], W=[pdenb])
                        P.op("pe", "matmul", pden[:, 0:1], PTt[par], onesb[:], start=False, stop=True,
                             R=[bf(f"PT{par}"), bf("onesb")], W=[pdenb])
                        d1, d1b = col()
                        P.op("dve", "tensor_scalar", d1, pden[:, 0:1], -1.0, 1.0, ALU.mult, ALU.max, R=[pdenb], W=[d1b])
                        d2, d2b = col()
                        P.op("dve", "tensor_tensor", d2, d1, pden[:, 0:1], ALU.max, R=[d1b, pdenb], W=[d2b])
                        rd, rdb = col()
                        P.op("dve", "reciprocal", rd, d2, R=[d2b], W=[rdb])
                        P.op("act", "activation", hraw[par], pnum[:, :], AF.Copy, scale=rd, R=[pnumb, rdb], W=[bf(f"hraw{par}")])
                        ss, ssb = col()
                        P.op("act", "activation", junk[:, :], hraw[par], AF.Square, accum_out=ss,
                             R=[bf(f"hraw{par}")], W=[bf("junk"), ssb])
                        rs, rsb = rstd_from(ss, ssb, 1.0 / 512.0)
                        P.op("dve", "scalar_tensor_tensor", tq, hraw[par], rs, wbc[:, h * 512:(h + 1) * 512], ALU.mult, ALU.mult,
                             R=[bf(f"hraw{par}"), rsb, bf("wbc")], W=[bf("tq")])
                        P.op("dve", "tensor_tensor", hA_tok[:, h * 512:(h + 1) * 512], tq, sig_o[:, i, h * 512:(h + 1) * 512],
                             ALU.mult, R=[bf("tq"), bf("sig_o")], W=[bf("hA_tok")])
                    for g in range(2):
                        pt, ptb = ptget()
                        for c in range(8):
                            cc_ = g * 8 + c
                            P.op("pe", "transpose", pt[:, c * 128:(c + 1) * 128], hA_tok[:, cc_ * 128:(cc_ + 1) * 128], ident[:],
                                 R=[bf("hA_tok"), bf("ident")], W=[ptb])
                        copy(evac_eng(), hAT[:, g * 8:(g + 1) * 8, tc_], pt[:, :].rearrange("p (c n) -> p c n", n=128),
                             R=[ptb], W=[bf("hAT")])
                    state_update(i, G)
                chk("mlstm")
                P.barrier()
                for g in range(4):
                    def ecc(m, ps, pb):
                        P.op("act", "activation", ccsb[:, m, :], ps, AF.Copy, R=[pb], W=[bf(f"ccsb{m}")])
                    formW(win_v, 0, KC, C_CC + g * 512, 512, lambda kc: hT[:, kc, 0:TT + 2], [hTB], TT + 2, ecc)

                    def ecx(m, ps, pb, g=g):
                        c = g * 4 + m
                        P.op("dve", "tensor_tensor", ubuf[:, m, :], ccsb[:, m, :], ps, ALU.mult, R=[pb, bf(f"ccsb{m}")], W=[bf(f"u{m}")])
                        P.op("dve", "tensor_scalar", cacc[:, m, :], ubuf[:, m, 2:TT + 2], convw[:, 32 + c:33 + c], None, ALU.mult,
                             R=[bf(f"u{m}"), bf("convw")], W=[bf(f"cacc{m}")])
                        P.op("dve", "scalar_tensor_tensor", cacc[:, m, :], ubuf[:, m, 1:TT + 1], convw[:, 16 + c:17 + c], cacc[:, m, :],
                             ALU.mult, ALU.add, R=[bf(f"u{m}"), bf("convw"), bf(f"cacc{m}")], W=[bf(f"cacc{m}")])
                        P.op("dve", "scalar_tensor_tensor", cacc[:, m, :], ubuf[:, m, 0:TT], convw[:, c:c + 1], cacc[:, m, :],
                             ALU.mult, ALU.add, R=[bf(f"u{m}"), bf("convw"), bf(f"cacc{m}")], W=[bf(f"cacc{m}")])
                    formW(win_v, 0, KC, C_CX + g * 512, 512, lambda kc: hT[:, kc, 0:TT + 2], [hTB], TT + 2, ecx)

                    def ecb(m, ps, pb, g=g):
                        P.op("dve", "tensor_tensor", hBT[:, g * 4 + m, :], cacc[:, m, :], ps, ALU.mult,
                             R=[pb, bf(f"cacc{m}")], W=[bf("hBT")])
                    formW(win_v, 0, KC, C_CB + g * 512, 512, lambda kc: hT[:, kc, 2:TT + 2], [hTB], TT, ecb)
                chk("conv")
                P.barrier()
                for j in range(8):
                    for n in range(2):
                        def eg(m, ps, pb):
                            P.op("act", "activation", gs[:, m, :], ps, AF.Sigmoid, R=[pb], W=[bf(f"gs{m}")])
                        formW(win_v, 0, KC, C_G + n * D + j * 512, 512, lambda kc: hT[:, kc, 2:TT + 2], [hTB], TT, eg)
                        src_h, src_b = (hAT, bf("hAT")) if n == 0 else (hBT, bf("hBT"))
                        if n == 0:
                            def ey(m, ps, pb):
                                P.op("dve", "tensor_tensor", mg[:, m, :], gs[:, m, :], ps, ALU.mult, R=[pb, bf(f"gs{m}")], W=[bf(f"mg{m}")])
                        else:
                            def ey(m, ps, pb, j=j):
                                P.op("dve", "tensor_tensor", gs[:, m, :], gs[:, m, :], ps, ALU.mult, R=[pb, bf(f"gs{m}")], W=[bf(f"gs{m}")])
                                P.op("dve", "tensor_tensor", mergedT[:, j * 4 + m, :], gs[:, m, :], mg[:, m, :], ALU.add,
                                     R=[bf(f"gs{m}"), bf(f"mg{m}")], W=[bf("mergedT")])
                        formW(br_v, n * 16, 16, j * 512, 512, lambda kc, s=src_h: s[:, kc, :], [src_b], TT, ey)
                halo_copy()
                P.barrier()

                ssq, ssqb = cols[:, 256:256 + NT * 8], bf("ssq")

                def evacY(j, first=True, last=True):
                    def f(i, ps, pb):
                        yb = bf(f"Y{i}")
                        if first:
                            P.op("dve", "tensor_copy", Yv[:, i, j * 512:(j + 1) * 512], ps, R=[pb], W=[yb])
                            if last:
                                P.op("act", "activation", junk[:, :], ps, AF.Square, accum_out=ssq[:, i * 8 + j:i * 8 + j + 1],
                                     R=[pb], W=[bf("junk"), ssqb])
                        else:
                            P.op("dve", "tensor_tensor", Yv[:, i, j * 512:(j + 1) * 512], ps, Yv[:, i, j * 512:(j + 1) * 512], ALU.add,
                                 R=[pb, yb], W=[yb])
                            if last:
                                P.op("act", "activation", junk[:, :], Yv[:, i, j * 512:(j + 1) * 512], AF.Square,
                                     accum_out=ssq[:, i * 8 + j:i * 8 + j + 1], R=[yb], W=[bf("junk"), ssqb])
                    return f

                def post_norm(i):
                    s1, s1b = col()
                    P.op("dve", "tensor_reduce", s1, ssq[:, i * 8:(i + 1) * 8], AX.X, ALU.add, R=[ssqb], W=[s1b])
                    rs, rsb = rstd_from(s1, s1b, 1.0 / D)
                    P.op("dve", "scalar_tensor_tensor", Yv[:, i, :], Yv[:, i, :], rs, wbc[:], ALU.mult, ALU.mult,
                         R=[bf(f"Y{i}"), rsb, bf("wbc")], W=[bf(f"Y{i}")])

                for j in range(8):
                    formA(mo_v, 0, KC, j * 512, 512, lambda kc, i: mergedT[:, kc, i * 128:(i + 1) * 128], [bf("mergedT")], NT, evacY(j))
                P.barrier()
                load_wbc(1)
                for i in range(NT):
                    P.dma("sp", Rv[:, i, :], xl[tok0 + i * 128:tok0 + (i + 1) * 128, :], f"rin{i}", W=[bf(f"R{i}")])
                for i in range(NT):
                    post_norm(i)
                    P.op("dve", "tensor_tensor", Rv[:, i, :], Yv[:, i, :], Rv[:, i, :], ALU.add, R=[bf(f"Y{i}"), bf(f"R{i}")], W=[bf(f"R{i}")])
                    if debug:
                        P.dma("sp", dbg["dbg_x1"][tok0 + i * 128:tok0 + (i + 1) * 128, :], Rv[:, i, :], "dbg", R=[bf(f"R{i}")])
                P.barrier()

                chk("mixer")
                load_wbc(2)
                norm_T(lambda i: Rv[:, i, :], lambda i: bf(f"R{i}"), NT)
                for hh in range(4):
                    for g in range(2):
                        def eq2(m, ps, pb, g=g):
                            copy(evac_eng(), q2T[:, g * 4 + m, :], ps, R=[pb], W=[bf("q2T")])
                        formW(xq_v, 0, KC, hh * 1024 + g * 512, 512, lambda kc: hT[:, kc, 2:TT + 2], [hTB], TT, eq2)
                    for i in range(NT):
                        tc_ = slice(i * 128, (i + 1) * 128)
                        psc, pscb = psget()
                        for c in range(8):
                            P.op("pe", "matmul", psc[:, 0:256], q2T[:, c, tc_], kmT[:, hh * 8 + c, :], start=(c == 0), stop=(c == 7),
                                 R=[bf("q2T"), bf("kmT")], W=[pscb])
                        mx, mxb = col()
                        P.op("dve", "tensor_reduce", mx, psc[:, 0:256], AX.X, ALU.max, R=[pscb], W=[mxb])
                        nmx, nmxb = col()
                        P.op("dve", "tensor_scalar", nmx, mx, -1.0, None, ALU.mult, R=[mxb], W=[nmxb])
                        sm, smb = col()
                        P.op("act", "activation", p_f, psc[:, 0:256], AF.Exp, bias=nmx, accum_out=sm, R=[pscb, nmxb], W=[bf("p_f"), smb])
                        rsm, rsmb = col()
                        P.op("dve", "reciprocal", rsm, sm, R=[smb], W=[rsmb])
                        P.op("dve", "tensor_scalar", pn_b, p_f, rsm, None, ALU.mult, R=[bf("p_f"), rsmb], W=[bf("pn_b")])
                        pt, ptb = ptget()
                        for mb in range(2):
                            P.op("pe", "transpose", pt[:, mb * 128:(mb + 1) * 128], pn_b[:, mb * 128:(mb + 1) * 128], ident[:],
                                 R=[bf("pn_b"), bf("ident")], W=[ptb])
                        copy(evac_eng(), pT[:, :, tc_], pt[:, 0:256].rearrange("p (c n) -> p c n", n=128), R=[ptb], W=[bf("pT")])
                    for eb_i in range(8):
                        po, pob = psget()
                        for mb in range(2):
                            e0 = hh * 1024 + eb_i * 128
                            P.op("pe", "matmul", po[:, 0:TT], vm[:, mb, e0:e0 + 128], pT[:, mb, :], start=(mb == 0), stop=(mb == 1),
                                 R=[bf("vm"), bf("pT")], W=[pob])
                        copy(evac_eng(), oT[:, hh * 8 + eb_i, :], po[:, 0:TT], R=[pob], W=[bf("oT")])
                P.barrier()
                for j in range(8):
                    formA(xo_v, 0, KC, j * 512, 512, lambda kc, i: oT[:, kc, i * 128:(i + 1) * 128], [bf("oT")], NT, evacY(j))
                load_wbc(3)
                for i in range(NT):
                    post_norm(i)
                    P.op("dve", "tensor_tensor", Rv[:, i, :], Yv[:, i, :], Rv[:, i, :], ALU.add, R=[bf(f"Y{i}"), bf(f"R{i}")], W=[bf(f"R{i}")])
                    if debug:
                        P.dma("sp", dbg["dbg_x2"][tok0 + i * 128:tok0 + (i + 1) * 128, :], Rv[:, i, :], "dbg", R=[bf(f"R{i}")])
                P.barrier()

                chk("xattn")
                load_wbc(5)
                norm_T(lambda i: Rv[:, i, :], lambda i: bf(f"R{i}"), NT)
                parts = [(c0, min(8, FC - c0)) for c0 in range(0, FC, 8)]
                for pi, (c0, nch) in enumerate(parts):
                    ap_ = aTp[pi % 2]
                    apb = bf(f"aT{pi % 2}")
                    for g0 in range(0, nch, 4):
                        nm_ = min(4, nch - g0)
                        sg_ = sgt[(g0 // 4) % 2]
                        sgb = bf(f"sg{(g0 // 4) % 2}")

                        def egt(m, ps, pb, sg_=sg_, sgb=sgb):
                            P.op("act", "activation", sg_[:, m, :], ps, AF.Silu, R=[pb], W=[sgb])

                        def eup(m, ps, pb, sg_=sg_, sgb=sgb, g0=g0, ap_=ap_, apb=apb):
                            P.op("dve", "tensor_tensor", ap_[:, g0 + m, :], sg_[:, m, :], ps, ALU.mult, R=[pb, sgb], W=[apb])
                        formW(fg_v, 0, KC, (c0 + g0) * 128, nm_ * 128, lambda kc: hT[:, kc, 2:TT + 2], [hTB], TT, egt)
                        formW(fu_v, 0, KC, (c0 + g0) * 128, nm_ * 128, lambda kc: hT[:, kc, 2:TT + 2], [hTB], TT, eup)
                    for j in range(8):
                        formA(fd_v, c0, nch, j * 512, 512, lambda kc, i, ap_=ap_: ap_[:, kc, i * 128:(i + 1) * 128], [apb], NT,
                              evacY(j, first=(pi == 0), last=(pi == len(parts) - 1)))
                load_wbc(6)
                for i in range(NT):
                    post_norm(i)
                    P.op("dve", "tensor_tensor", Yv[:, i, :], Yv[:, i, :], Rv[:, i, :], ALU.add, R=[bf(f"Y{i}"), bf(f"R{i}")], W=[bf(f"Y{i}")])
                    od = P.dma("sp", out[tok0 + i * 128:tok0 + (i + 1) * 128, :], Yv[:, i, :], "out", R=[bf(f"Y{i}")])
                    out_dmas.append(od)
            return out_dmas

        plan = []
        wcache = {"idx": {}, "t": None, "pre": []}
        schedule(Prog(True), plan, 0)
        cnt = {}
        for (_, _, _, key) in plan:
            cnt[key] = cnt.get(key, 0) + 1
        for (_, _, _, key) in plan:
            if cnt[key] > 1 and key not in wcache["idx"]:
                wcache["idx"][key] = len(wcache["idx"])
        seen_pre = set(k for (_, _, _, k) in plan[:wcache.get("local0", 0)])
        pre = []
        for ent in plan[wcache.get("local0", len(plan)):]:
            k = ent[3]
            if k in wcache["idx"] and k not in seen_pre:
                seen_pre.add(k)
                pre.append(ent)
        wcache["pre"] = pre[:128] if NPRE >= 8 else pre[:8]
        if wcache["idx"]:
            nt_ = len(wcache["idx"])
            chunks = [nc.dram_tensor(f"wcache{c}", [min(192, nt_ - c * 192), 128, 4096], BF16, kind="Internal").ap()
                      for c in range((nt_ + 191) // 192)]

            class _Cache:
                def __getitem__(self, i):
                    return chunks[i // 192][i % 192]
            wcache["t"] = _Cache()
        P = Prog(False)
        outs = schedule(P, plan, len(plan))
        fw = {"sp": [outs[-1]] + ([P.last_dma["dbg"]] if debug else [])}
        P.emit(nc, st, fw)
    return nc


def _host_params(inp):
    f = lambda a: np.ascontiguousarray(np.asarray(a, dtype=np.float32))
    nws = [inp["norm_pre_mix"][0], inp["norm_post_mix"][0], inp["norm_pre_xattn"][0], inp["norm_post_xattn"][0],
           inp["norm_mem"][0], inp["norm_pre_ffn"][0], inp["norm_post_ffn"][0]]
    nw_bc = f(np.broadcast_to(np.stack([np.asarray(v) for v in nws])[:, None, :], (7, 128, D)))
    hnw_bc = f(np.broadcast_to(np.asarray(inp["mlstm_head_norm"][0])[None, :], (128, 2048)))
    bif_bc = f(np.broadcast_to(np.asarray(inp["b_if"][0]).reshape(1, 8), (128, 8)))
    cw = np.asarray(inp["conv_w"][0])
    convw_c = f(cw.reshape(3, 16, 128).transpose(2, 0, 1).reshape(128, 48))
    return dict(
        w_in=f(inp["w_in"][0]), w_br=f(np.asarray(inp["w_branch"][0]).reshape(D, D)), w_mo=f(inp["w_mix_out"][0]),
        w_xq=f(inp["w_xq"][0]), w_xk=f(inp["w_xk"][0]), w_xv=f(inp["w_xv"][0]), w_xo=f(inp["w_xo"][0]),
        w_fg=f(inp["w_ffn_gate"][0]), w_fu=f(inp["w_ffn_up"][0]), w_fd=f(inp["w_ffn_down"][0]),
        nw_bc=nw_bc, hnw_bc=hnw_bc, bif_bc=bif_bc, convw_c=convw_c)


def kernel(**inputs):
    x = np.asarray(inputs["x"], dtype=np.float32)
    mem = np.asarray(inputs["mem"], dtype=np.float32)
    Bsz, S, _ = x.shape
    NSEG = 8 // Bsz
    SEG = S // NSEG
    NLOC = SEG // 128
    NPRE = (S - SEG) // 128
    common = _host_params(inputs)
    nc = build(NLOC, NPRE, 2)
    in_maps = []
    for c in range(8):
        b, s = c // NSEG, c % NSEG
        xpre = np.zeros((NPRE * 128, D), np.float32)
        if s > 0:
            xpre[NPRE * 128 - s * SEG:] = x[b, :s * SEG]
        m = dict(common)
        m["xl"] = np.ascontiguousarray(x[b, s * SEG:(s + 1) * SEG])
        m["xp"] = xpre
        m["memb"] = np.ascontiguousarray(mem[b])
        in_maps.append(m)
    res = run_bass_kernel_spmd(nc, in_maps, core_ids=list(range(8)))
    outp = np.empty((Bsz, S, D), np.float32)
    for c in range(8):
        b, s = c // NSEG, c % NSEG
        outp[b, s * SEG:(s + 1) * SEG] = res.results[c]["out"]
    return outp
```

```python
import contextlib
import numpy as np
import concourse.bass as bass
import concourse.mybir as mybir
from concourse.bass_utils import run_bass_kernel_spmd

F32 = mybir.dt.float32
BF16 = mybir.dt.bfloat16
AF = mybir.ActivationFunctionType
ALU = mybir.AluOpType
AX = mybir.AxisListType

D = 4096
DFF = 11008
KC = D // 128
FC = DFF // 128
C_Q, C_K, C_V, C_O, C_IF, C_CB, C_CC, C_CX, C_G = 0, 1024, 2048, 4096, 6144, 6152, 8200, 10248, 12296
EPS = 1e-6
NW = 3


class Ins:
    __slots__ = ("eng", "meth", "args", "kw", "deps", "done", "marked", "key")

    def __init__(self, eng, meth, args, kw, deps, key=None):
        self.eng, self.meth, self.args, self.kw = eng, meth, args, kw
        self.deps = deps
        self.done = None
        self.marked = False
        self.key = key
        for d in deps:
            d.marked = True


class Buf:
    __slots__ = ("w", "r", "name", "excl")

    def __init__(self, name="", excl=False):
        self.w = {}
        self.r = {}
        self.name = name
        self.excl = excl


class Prog:
    ENG = ("pe", "act", "dve", "pool", "sp")

    def __init__(self, dry):
        self.dry = dry
        self.q = {e: [] for e in self.ENG}
        self.dma_cnt = {}
        self.fence = []
        self.last = {}
        self.last_dma = {}
        self.dummy = Ins("x", "x", (), {}, [])

    def _mk(self, eng, meth, args, kw, deps, key=None):
        dd = []
        seen = set()
        for d in deps:
            if d is None or d is self.dummy:
                continue
            if d.key is None and d.eng == eng and eng == "pe":
                continue
            if id(d) in seen:
                continue
            seen.add(id(d))
            dd.append(d)
        i = Ins(eng, meth, args, kw, dd, key=key)
        self.q[eng].append(i)
        return i

    def op(self, eng, meth, *args, R=(), W=(), deps=(), **kw):
        if self.dry:
            return self.dummy
        d = list(deps)
        for f in self.fence:
            if f.key is not None or f.eng != eng:
                d.append(f)
        for b in R:
            d.extend(b.w.values())
            if b.excl:
                for k, v in b.r.items():
                    if k != eng:
                        d.append(v)
        for b in W:
            for k, v in b.w.items():
                if k != eng:
                    d.append(v)
            for k, v in b.r.items():
                if k != eng:
                    d.append(v)
        i = self._mk(eng, meth, args, kw, d)
        for b in R:
            b.r[eng] = i
        for b in W:
            b.r = {}
            b.w[eng] = i
        self.last[eng] = i
        return i

    def dma(self, eng, out, in_, key, R=(), W=(), deps=(), nofence=False):
        if self.dry:
            return self.dummy
        d = list(deps) + ([] if nofence else list(self.fence))
        for b in R:
            d.extend(b.w.values())
        for b in W:
            d.extend(b.w.values())
            d.extend(b.r.values())
        i = self._mk(eng, "dma_start", (), dict(out=out, in_=in_), d, key=key)
        self.dma_cnt[key] = self.dma_cnt.get(key, 0) + 1
        i.done = (("dma", key), 16 * self.dma_cnt[key])
        for b in R:
            b.r[key] = i
        for b in W:
            b.r = {}
            b.w[key] = i
        if eng == "sp" and not nofence:
            self.last_dma[key] = i
        return i

    def barrier(self):
        if self.dry:
            return
        self.fence = [self.last[e] for e in ("pe", "act", "dve") if e in self.last] + list(self.last_dma.values())

    def emit(self, nc, st, final_waits):
        for e in self.ENG:
            c = 0
            for i in self.q[e]:
                if i.key is None and i.marked:
                    c += 1
                    i.done = (("eng", e), c)
        sems = {}
        for e in ("pe", "act", "dve", "pool"):
            sems[("eng", e)] = st.enter_context(nc.semaphore("s_" + e))
        for k in self.dma_cnt:
            sems[("dma", k)] = st.enter_context(nc.semaphore("d_" + str(k)))
        block = st.enter_context(nc.Block())
        prog = self

        def run(engname, eng):
            waited = {}
            for i in prog.q[engname]:
                for d in i.deps:
                    s, v = d.done
                    if waited.get(s, 0) < v:
                        eng.wait_ge(sems[s], v)
                        waited[s] = v
                bi = getattr(eng, i.meth)(*i.args, **i.kw)
                if i.key is not None:
                    bi.then_inc(sems[("dma", i.key)], 16)
                elif i.marked:
                    bi.then_inc(sems[i.done[0]], 1)
            for d in final_waits.get(engname, ()):
                s, v = d.done
                if waited.get(s, 0) < v:
                    eng.wait_ge(sems[s], v)
                    waited[s] = v

        @block.tensor
        def _(eng):
            run("pe", eng)

        @block.scalar
        def _(eng):
            run("act", eng)

        @block.vector
        def _(eng):
            run("dve", eng)

        @block.gpsimd
        def _(eng):
            run("pool", eng)

        @block.sync
        def _(eng):
            run("sp", eng)


class _Stop(Exception):
    pass


def build(NLOC, NPRE, NT, debug=False, stop=None):
    TT = 128 * NT
    assert NLOC % NT == 0 and NPRE % NT == 0 and NT == 2
    nc = bass.Bass("TRN2", target_bir_lowering=False)

    def din(name, shape):
        return nc.dram_tensor(name, shape, F32, kind="ExternalInput").ap()

    xl = din("xl", [NLOC * 128, D])
    xp = din("xp", [NPRE * 128, D])
    memb = din("memb", [256, D])
    w_in = din("w_in", [D, 20488])
    w_br = din("w_br", [D, D])
    w_mo = din("w_mo", [D, D])
    w_xq = din("w_xq", [D, D])
    w_xk = din("w_xk", [D, D])
    w_xv = din("w_xv", [D, D])
    w_xo = din("w_xo", [D, D])
    w_fg = din("w_fg", [D, DFF])
    w_fu = din("w_fu", [D, DFF])
    w_fd = din("w_fd", [DFF, D])
    nw_bc = din("nw_bc", [7, 128, D])
    hnw_bc = din("hnw_bc", [128, 2048])
    bif_bc = din("bif_bc", [128, 8])
    convw_c = din("convw_c", [128, 48])
    out = nc.dram_tensor("out", [NLOC * 128, D], F32, kind="ExternalOutput").ap()
    dbg = {}
    if debug:
        for nm_ in ("dbg_x1", "dbg_x2"):
            dbg[nm_] = nc.dram_tensor(nm_, [NLOC * 128, D], F32, kind="ExternalOutput").ap()

    class WV:
        def __init__(self, ap, name):
            self.ap, self.name = ap, name

    def wview(w, name):
        return WV(w.rearrange("(kc p) n -> p kc n", p=128), name)

    st = contextlib.ExitStack()
    with st:
        def sb(name, shape, dt):
            return st.enter_context(nc.sbuf_tensor(name, shape, dt))

        ident = sb("ident", [128, 128], BF16)
        identf = sb("identf", [128, 128], F32)
        tri = sb("tri", [128, 128], F32)
        nmask = sb("nmask", [128, 128], F32)
        ones = sb("ones", [128, 128], F32)
        zeros = identf
        onesb = sb("onesb", [128, 1], BF16)
        epsc = sb("epsc", [128, 1], F32)
        bif = sb("bif", [128, 8], F32)
        convw = sb("convw", [128, 48], F32)
        wgate = sb("wgate", [128, KC, 8], BF16)
        cols = sb("cols", [128, 256 + 16], F32)
        gcols = sb("gcols", [128, 2, 96], F32)
        junk = sb("junk", [128, 512], BF16)
        wbc = sb("wbc", [128, D], F32)
        Cst = sb("Cst", [128, 8, 512], F32)
        Cbf = sb("Cbf", [128, 8, 512], BF16)
        nst = sb("nst", [128, 8], F32)
        nbf = sb("nbf", [128, 8], BF16)
        kmT = sb("kmT", [128, KC, 256], BF16)
        vm = sb("vm", [128, 2, D], BF16)
        hT = sb("hT", [128, KC, TT + 2], BF16)
        wslots = [sb(f"wslot{i}", [128, 8, 512], BF16) for i in range(NW)]
        arA = sb("arA", [128, NT * D], F32)
        arB = sb("arB", [128, NT * D], F32)
        arC = sb("arC", [128, 5120], F32)
        psF = [st.enter_context(nc.psum_tensor(f"psF{i}", [128, 512], F32)) for i in range(6)]
        psT = [st.enter_context(nc.psum_tensor(f"psT{i}", [128, 1024], BF16)) for i in range(2)]

        def bfv(ar, w0, w1, inner=None):
            v = ar[:, w0:w1].bitcast(BF16)
            if inner is not None:
                v = v.rearrange("p (c n) -> p c n", n=inner)
            return v

        def f3(ar, w0, w1, inner):
            return ar[:, w0:w1].rearrange("p (c n) -> p c n", n=inner)

        Rv = f3(arA, 0, NT * D, D)
        Yv = f3(arB, 0, NT * D, D)
        k_tok = bfv(arB, 0, NT * 512, 1024)
        v_tok = bfv(arB, NT * 512, NT * 1536, 2048)
        o0 = NT * 1536
        qT = bfv(arB, o0, o0 + 4 * TT, TT)
        kT = bfv(arB, o0 + 4 * TT, o0 + 8 * TT, TT)
        o1 = o0 + 8 * TT
        gs = f3(arB, o1, o1 + 4 * TT, TT)
        mg = f3(arB, o1 + 4 * TT, o1 + 8 * TT, TT)
        assert o1 + 8 * TT <= NT * D
        p_f = arB[:, 0:256]
        pn_b = bfv(arB, 256, 384)
        pT = bfv(arB, 384, 384 + TT, TT)
        sig_o = f3(arA, 0, NT * 2048, 2048)
        a0 = NT * 2048
        lfrep = f3(arA, a0, a0 + 512, 128)
        Ebc = arA[:, a0 + 512:a0 + 1024]
        Dm = [arA[:, a0 + 1024 + i * 128:a0 + 1152 + i * 128] for i in range(2)]
        Wt = [arA[:, a0 + 1280 + i * 128:a0 + 1408 + i * 128] for i in range(2)]
        PTt = [bfv(arA, a0 + 1536 + i * 64, a0 + 1600 + i * 64) for i in range(2)]
        qp = [bfv(arA, a0 + 1664 + i * 128, a0 + 1792 + i * 128, 128) for i in range(2)]
        hraw = [arA[:, a0 + 1920 + i * 512:a0 + 2432 + i * 512] for i in range(2)]
        tq = arA[:, a0 + 2944:a0 + 3456]
        kw = bfv(arA, a0 + 3456, a0 + 3968)
        assert a0 + 3968 <= NT * D
        ccsb = f3(arA, 0, 4 * (TT + 2), TT + 2)
        ubuf = f3(arA, 4 * (TT + 2), 8 * (TT + 2), TT + 2)
        cacc = f3(arA, 8 * (TT + 2), 8 * (TT + 2) + 4 * TT, TT)
        assert 8 * (TT + 2) + 4 * TT <= a0
        mergedT = bfv(arA, a0, a0 + 16 * TT, TT)
        assert a0 + 16 * TT <= NT * D
        xn = bfv(arC, 0, 2048)
        hBT = bfv(arC, 0, 8 * TT, TT)
        hAT = bfv(arC, 2048, 2048 + 8 * TT, TT)
        hA_tok = bfv(arC, 4096, 5120)
        oT = bfv(arC, 0, 16 * TT, TT)
        q2T = bfv(arC, 4096, 4096 + 4 * TT, TT)
        aTp = [bfv(arC, i * 4 * TT, (i + 1) * 4 * TT, TT) for i in range(2)]
        sgt = [f3(arC, 2048 + i * 4 * TT, 2048 + (i + 1) * 4 * TT, TT) for i in range(2)]
        assert 2048 + 8 * TT <= 5120 and 16 * TT <= 4096

        win_v = wview(w_in, "in")

        def schedule(P, plan, nplanned):
            try:
                return schedule_(P, plan, nplanned)
            except _Stop:
                P.barrier()
                od = P.dma("sp", out[0:128, :], arB[:, 0:D], "out")
                if debug:
                    P.dma("sp", dbg["dbg_x1"][0:128, :], arA[:, 0:D], "dbg")
                return [od]

        def chk(name):
            if stop == name:
                raise _Stop()

        def schedule_(P, plan, nplanned):
            B = {}

            def bf(name):
                if name not in B:
                    B[name] = Buf(name)
                return B[name]

            bankF = [Buf(f"psF{i}", True) for i in range(6)]
            bankT = [Buf(f"psT{i}", True) for i in range(2)]
            rr = {"f": 0, "t": 0, "e": 0}

            def psget():
                b = rr["f"] % 5
                rr["f"] += 1
                return psF[b], bankF[b]

            def ptget():
                b = rr["t"] % 2
                rr["t"] += 1
                return psT[b], bankT[b]

            def evac_eng():
                rr["e"] += 1
                return "act" if rr["e"] % 2 else "dve"

            def copy(eng, out_, in_, R, W, scale=None):
                if eng == "act":
                    if scale is None:
                        return P.op("act", "activation", out_, in_, AF.Copy, R=R, W=W)
                    return P.op("act", "activation", out_, in_, AF.Copy, scale=scale, R=R, W=W)
                if scale is None:
                    return P.op("dve", "tensor_copy", out_, in_, R=R, W=W)
                return P.op("dve", "tensor_scalar", out_, in_, scale, None, ALU.mult, R=R, W=W)

            wsb = [Buf(f"ws{i}") for i in range(NW)]
            wstate = {"next_get": 0, "next_load": 0}

            wb_ins = {}
            phase = {"prefix": False}
            stg = [bfv(arA, 0, 2048, 512), bfv(arA, 2048, 4096, 512), bfv(arA, 4608, 6656, 512)]
            stgb = [Buf(f"stg{j}") for j in range(3)]
            cvs = {"n": 0, "tick": 0}

            def preconvert():
                if cvs["n"] >= len(wcache["pre"]):
                    return
                src, nk, ncols, key = wcache["pre"][cvs["n"]]
                j = cvs["n"] % 3
                cvs["n"] += 1
                P.dma("pool", stg[j][:, 0:nk, 0:ncols], src, f"cv{j}", W=[stgb[j]], nofence=True)
                wb = P.dma("sp", wcache["t"][wcache["idx"][key]], stg[j][:].rearrange("p k n -> p (k n)"), f"cwb{j}",
                           R=[stgb[j]], nofence=True)
                wb_ins[key] = wb
                P.last_dma[f"cwb{j}"] = wb

            def wload(m):
                src, nk, ncols, key = plan[m]
                s_ = m % NW
                flat = wslots[s_][:].rearrange("p k n -> p (k n)")
                if key in wb_ins:
                    P.dma("sp", flat, wcache["t"][wcache["idx"][key]], f"w{s_}", W=[wsb[s_]], deps=[wb_ins[key]], nofence=True)
                    if phase["prefix"]:
                        cvs["tick"] += 1
                        if cvs["tick"] % 5 < 3:
                            preconvert()
                    return
                P.dma("pool", wslots[s_][:, 0:nk, 0:ncols], src, f"w{s_}", W=[wsb[s_]], nofence=True)
                if key in wcache["idx"]:
                    wb_ins[key] = P.dma("sp", wcache["t"][wcache["idx"][key]], flat, f"wb{s_}", R=[wsb[s_]], nofence=True)
                if phase["prefix"]:
                    cvs["tick"] += 1
                    if cvs["tick"] % 5 < 3:
                        preconvert()

            def wget(src, nk, ncols, key):
                if P.dry:
                    plan.append((src, nk, ncols, key))
                    return wslots[0], wsb[0]
                n = wstate["next_get"]
                wstate["next_get"] += 1
                while wstate["next_load"] < min(n + NW, nplanned):
                    wload(wstate["next_load"])
                    wstate["next_load"] += 1
                return wslots[n % NW], wsb[n % NW]

            def formA(Wv, row0, K, col0, ncols, lhs_view, lhs_bufs, ntl, evac):
                banks = [psget() for _ in range(ntl)]
                for k0 in range(0, K, 8):
                    nk = min(8, K - k0)
                    wt, wb = wget(Wv.ap[:, row0 + k0:row0 + k0 + nk, col0:col0 + ncols], nk, ncols, (Wv.name, row0 + k0, col0, nk, ncols))
                    for kl in range(nk):
                        kc = k0 + kl
                        for i in range(ntl):
                            P.op("pe", "matmul", banks[i][0][:, 0:ncols], lhs_view(kc, i), wt[:, kl, 0:ncols],
                                 start=(kc == 0), stop=(kc == K - 1), R=[wb] + lhs_bufs, W=[banks[i][1]])
                for i in range(ntl):
                    evac(i, banks[i][0][:, 0:ncols], banks[i][1])

            def formW(Wv, row0, K, col0, ncols, rhs_view, rhs_bufs, N, evac):
                nm = ncols // 128
                banks = [psget() for _ in range(nm)]
                for k0 in range(0, K, 8):
                    nk = min(8, K - k0)
                    wt, wb = wget(Wv.ap[:, row0 + k0:row0 + k0 + nk, col0:col0 + ncols], nk, ncols, (Wv.name, row0 + k0, col0, nk, ncols))
                    for kl in range(nk):
                        kc = k0 + kl
                        for m in range(nm):
                            P.op("pe", "matmul", banks[m][0][:, 0:N], wt[:, kl, m * 128:(m + 1) * 128], rhs_view(kc),
                                 start=(kc == 0), stop=(kc == K - 1), R=[wb] + rhs_bufs, W=[banks[m][1]])
                for m in range(nm):
                    evac(m, banks[m][0][:, 0:N], banks[m][1])

            hTB = bf("hT")

            def hcols(i):
                return slice(2 + i * 128, 2 + (i + 1) * 128)

            P.op("pool", "memset", ones[:], 1.0, W=[bf("ones")])
            P.op("pool", "memset", onesb[:], 1.0, W=[bf("onesb")])
            P.op("pool", "memset", epsc[:], EPS, W=[bf("epsc")])
            P.op("pool", "memset", zeros[:], 0.0, W=[bf("identf")])
            P.op("pool", "affine_select", tri[:], ones[:], pattern=[[1, 128]], compare_op=ALU.is_ge, fill=0.0,
                 base=0, channel_multiplier=-1, R=[bf("ones")], W=[bf("tri")])
            P.op("pool", "affine_select", nmask[:], zeros[:], pattern=[[1, 128]], compare_op=ALU.is_ge, fill=-1e30,
                 base=0, channel_multiplier=-1, R=[bf("identf")], W=[bf("nmask")])
            P.op("pool", "affine_select", identf[:], ones[:], pattern=[[-1, 128]], compare_op=ALU.is_equal, fill=0.0,
                 base=0, channel_multiplier=1, R=[bf("ones"), bf("nmask")], W=[bf("identf")])
            P.op("dve", "tensor_copy", ident[:], identf[:], R=[bf("identf")], W=[bf("ident")])
            P.op("dve", "memset", Cst[:], 0.0, W=[bf("Cst")])
            P.op("dve", "memset", Cbf[:], 0.0, W=[bf("Cbf")])
            P.op("dve", "memset", nst[:], 0.0, W=[bf("nst")])
            P.op("dve", "memset", nbf[:], 0.0, W=[bf("nbf")])
            P.op("dve", "memset", hT[:], 0.0, W=[hTB])
            P.dma("sp", bif[:], bif_bc, "small0", W=[bf("bif")])
            P.dma("sp", convw[:], convw_c, "small1", W=[bf("convw")])
            P.dma("pool", wgate[:], win_v.ap[:, :, C_IF:C_IF + 8], "wgate", W=[bf("wgate")])
            cidx = {"n": 0}

            def col(n=1):
                sl = cidx["n"] % 32
                cidx["n"] += 1
                return cols[:, sl * 8:sl * 8 + n], bf(f"cs{sl}")

            gidx = {"n": 0, "par": 0}

            def gcol(n):
                sl = gidx["n"]
                gidx["n"] += 1
                assert sl < 12
                return gcols[:, gidx["par"], sl * 8:sl * 8 + n], bf(f"gc{gidx['par']}_{sl}")

            def load_wbc(idx):
                P.dma("sp", wbc[:], nw_bc[idx], "wbc", W=[bf("wbc")])

            def rstd_from(ss_ap, ss_b, scale):
                r_ap, r_b = col()
                P.op("act", "activation", r_ap, ss_ap, AF.Sqrt, bias=epsc[:, 0:1], scale=scale, R=[ss_b, bf("epsc")], W=[r_b])
                r2, r2b = col()
                P.op("dve", "reciprocal", r2, r_ap, R=[r_b], W=[r2b])
                return r2, r2b

            def norm_T(src_view, src_buf, ntiles):
                for i in range(ntiles):
                    ss, ssb = col()
                    P.op("act", "activation", xn[:, :], src_view(i), AF.Square, accum_out=ss, R=[src_buf(i)], W=[bf("xn"), ssb])
                    rs, rsb = rstd_from(ss, ssb, 1.0 / D)
                    P.op("dve", "scalar_tensor_tensor", xn[:, :], src_view(i), rs, wbc[:], ALU.mult, ALU.mult,
                         R=[src_buf(i), rsb, bf("wbc")], W=[bf("xn")])
                    for g in range(4):
                        pt, ptb = ptget()
                        for c in range(8):
                            kc = g * 8 + c
                            P.op("pe", "transpose", pt[:, c * 128:(c + 1) * 128], xn[:, kc * 128:(kc + 1) * 128], ident[:],
                                 R=[bf("xn"), bf("ident")], W=[ptb])
                        copy(evac_eng(), hT[:, g * 8:(g + 1) * 8, hcols(i)], pt[:, :].rearrange("p (c n) -> p c n", n=128),
                             R=[ptb], W=[hTB])

            def lhsH(kc, i):
                return hT[:, kc, hcols(i)]

            def gates(i, local):
                gidx["par"] ^= 1
                gidx["n"] = 0
                col_ = gcol
                ps, pb = psget()
                for kc in range(KC):
                    P.op("pe", "matmul", ps[:, 0:8], lhsH(kc, i), wgate[:, kc, :], start=(kc == 0), stop=(kc == KC - 1),
                         R=[hTB, bf("wgate")], W=[pb])
                g, gb = col_(8)
                P.op("dve", "tensor_tensor", g, ps[:, 0:8], bif[:], ALU.add, R=[pb, bf("bif")], W=[gb])
                t, tb = col_(8)
                P.op("act", "activation", t, g, AF.Tanh, scale=1.0 / 15.0, R=[gb], W=[tb])
                li, lib = col_(4)
                P.op("dve", "tensor_scalar", li, t[:, 0:4], 15.0, None, ALU.mult, R=[tb], W=[lib])
                e, eb_ = col_(4)
                P.op("act", "activation", e, t[:, 4:8], AF.Exp, scale=-15.0, R=[tb], W=[eb_])
                sp_, spb = col_(4)
                P.op("act", "activation", sp_, e, AF.Ln, bias=1.0, R=[eb_], W=[spb])
                lf, lfb = col_(4)
                P.op("dve", "tensor_scalar", lf, sp_, -1.0, None, ALU.mult, R=[spb], W=[lfb])
                for h in range(4):
                    P.op("dve", "tensor_scalar", lfrep[:, h, :], ones[:], lf[:, h:h + 1], None, ALU.mult,
                         R=[lfb, bf("ones")], W=[bf("lfrep")])
                ps2, pb2 = psget()
                P.op("pe", "matmul", ps2[:, 0:4], tri[:], lf, start=True, stop=True, R=[bf("tri"), lfb], W=[pb2])
                bcol, bcb = col_(4)
                P.op("dve", "tensor_copy", bcol, ps2[:, 0:4], R=[pb2], W=[bcb])
                ps3, pb3 = psF[5], bankF[5]
                for h in range(4):
                    P.op("pe", "matmul", ps3[:, h * 128:(h + 1) * 128], lfrep[:, h, :], tri[:], start=True, stop=True,
                         R=[bf("lfrep"), bf("tri")], W=[pb3])
                btot = ps3[:, :].rearrange("p (h l) -> p h l", l=128)[:, :, 127]
                a, ab = col_(4)
                P.op("dve", "tensor_tensor", a, btot, bcol, ALU.subtract, R=[pb3, bcb], W=[ab])
                a2, a2b = col_(4)
                P.op("dve", "tensor_tensor", a2, a, li, ALU.add, R=[ab, lib], W=[a2b])
                aw, awb = col_(4)
                P.op("act", "activation", aw, a2, AF.Exp, R=[a2b], W=[awb])
                ebt, ebb = col_(4)
                P.op("act", "activation", ebt, btot, AF.Exp, R=[pb3], W=[ebb])
                if local:
                    P.op("act", "activation", Ebc, ps3[:, :], AF.Exp, R=[pb3], W=[bf("Ebc")])
                return dict(li=li, lib=lib, bcol=bcol, bcb=bcb, ps3=ps3, pb3=pb3, aw=aw, awb=awb, eb=ebt, ebb=ebb)

            def state_update(i, G):
                ktb, vtb = bf("k_tok"), bf("v_tok")
                for h in range(4):
                    P.op("dve", "tensor_scalar", kw[:, h * 256:(h + 1) * 256], k_tok[:, i, h * 256:(h + 1) * 256],
                         G["aw"][:, h:h + 1], None, ALU.mult, R=[ktb, G["awb"]], W=[bf("kw")])
                eb8, eb8b = col(8)
                e8v = eb8.rearrange("p (h j) -> p h j", j=2)
                for j in range(2):
                    P.op("dve", "tensor_copy", e8v[:, :, j], G["eb"], R=[G["ebb"]], W=[eb8b])
                for c in range(8):
                    h = c // 2
                    ps, pb = psget()
                    P.op("pe", "matmul", ps[:, :], kw[:, c * 128:(c + 1) * 128], v_tok[:, i, h * 512:(h + 1) * 512],
                         start=True, stop=True, R=[bf("kw"), vtb], W=[pb])
                    P.op("dve", "scalar_tensor_tensor", Cst[:, c, :], Cst[:, c, :], G["eb"][:, h:h + 1], ps[:, :],
                         ALU.mult, ALU.add, R=[pb, G["ebb"], bf("Cst")], W=[bf("Cst")])
                    P.op("act", "activation", Cbf[:, c, :], Cst[:, c, :], AF.Copy, R=[bf("Cst")], W=[bf("Cbf")])
                psn, pnb = psget()
                for c in range(8):
                    P.op("pe", "matmul", psn[:, c:c + 1], kw[:, c * 128:(c + 1) * 128], onesb[:], start=True, stop=True,
                         R=[bf("kw"), bf("onesb")], W=[pnb])
                t8, t8b = col(8)
                P.op("dve", "tensor_tensor", t8, nst[:], eb8, ALU.mult, R=[bf("nst"), eb8b], W=[t8b])
                P.op("dve", "tensor_tensor", nst[:], t8, psn[:, 0:8], ALU.add, R=[t8b, pnb], W=[bf("nst")])
                P.op("dve", "tensor_copy", nbf[:], nst[:], R=[bf("nst")], W=[bf("nbf")])

            def evac_tok(dst3, gbase):
                def f(i, ps, pb):
                    copy(evac_eng(), dst3[:, i, gbase:gbase + 512], ps, R=[pb], W=[bf("tokdst")])
                return f

            def proj_kv(ntl):
                ktb, vtb = bf("k_tok"), bf("v_tok")

                def ek(g):
                    def f(i, ps, pb):
                        copy(evac_eng(), k_tok[:, i, g * 512:(g + 1) * 512], ps, R=[pb], W=[ktb])
                    return f

                def ev(g):
                    def f(i, ps, pb):
                        copy(evac_eng(), v_tok[:, i, g * 512:(g + 1) * 512], ps, R=[pb], W=[vtb])
                    return f
                for g in range(2):
                    formA(win_v, 0, KC, C_K + g * 512, 512, lhsH, [hTB], ntl, ek(g))
                for g in range(4):
                    formA(win_v, 0, KC, C_V + g * 512, 512, lhsH, [hTB], ntl, ev(g))

            def halo_copy():
                P.op("dve", "tensor_copy", hT[:, :, 0:2], hT[:, :, TT:TT + 2], R=[hTB], W=[hTB])

            chk("setup")
            load_wbc(4)
            for i in range(2):
                P.dma("sp", Yv[:, i, :], memb[i * 128:(i + 1) * 128, :], f"xin{i}", W=[bf(f"Y{i}")])
            norm_T(lambda i: Yv[:, i, :], lambda i: bf(f"Y{i}"), 2)
            xk_v, xv_v = wview(w_xk, "xk"), wview(w_xv, "xv")
            for g in range(8):
                def ekm(m, ps, pb, g=g):
                    copy(evac_eng(), kmT[:, g * 4 + m, :], ps, R=[pb], W=[bf("kmT")], scale=1.0 / 32.0)
                formW(xk_v, 0, KC, g * 512, 512, lambda kc: hT[:, kc, 2:258], [hTB], 256, ekm)
            for g in range(8):
                def evm(i, ps, pb, g=g):
                    copy(evac_eng(), vm[:, i, g * 512:(g + 1) * 512], ps, R=[pb], W=[bf("vm")])
                formA(xv_v, 0, KC, g * 512, 512, lhsH, [hTB], 2, evm)
            P.barrier()
            P.op("dve", "memset", hT[:, :, 0:2], 0.0, W=[hTB])
            chk("mem")

            phase["prefix"] = True
            for pg in range(NPRE // NT):
                P.barrier()
                if pg == 0:
                    load_wbc(0)
                for i in range(NT):
                    r0 = (pg * NT + i) * 128
                    P.dma("sp", Yv[:, i, :], xp[r0:r0 + 128, :], f"xin{i}", W=[bf(f"Y{i}")])
                norm_T(lambda i: Yv[:, i, :], lambda i: bf(f"Y{i}"), NT)
                P.barrier()
                proj_kv(NT)
                for i in range(NT):
                    G = gates(i, False)
                    state_update(i, G)
                halo_copy()

            phase["prefix"] = False
            if P.dry:
                wcache["local0"] = len(plan)
            chk("prefix")
            br_v, mo_v, xq_v, xo_v = wview(w_br, "br"), wview(w_mo, "mo"), wview(w_xq, "xq"), wview(w_xo, "xo")
            fg_v, fu_v, fd_v = wview(w_fg, "fg"), wview(w_fu, "fu"), wview(w_fd, "fd")
            out_dmas = []
            for ps_i in range(NLOC // NT):
                tok0 = ps_i * TT
                P.barrier()
                load_wbc(0)
                for i in range(NT):
                    P.dma("sp", Yv[:, i, :], xl[tok0 + i * 128:tok0 + (i + 1) * 128, :], f"xin{i}", W=[bf(f"Y{i}")])
                norm_T(lambda i: Yv[:, i, :], lambda i: bf(f"Y{i}"), NT)
                P.barrier()
                P.dma("sp", wbc[:, 0:2048], hnw_bc, "wbc", W=[bf("wbc")])
                proj_kv(NT)
                for g in range(4):
                    def eo(i, ps, pb, g=g):
                        P.op("act", "activation", sig_o[:, i, g * 512:(g + 1) * 512], ps, AF.Sigmoid, R=[pb], W=[bf("sig_o")])
                    formA(win_v, 0, KC, C_O + g * 512, 512, lhsH, [hTB], NT, eo)
                for g in range(2):
                    def eq(m, ps, pb, g=g):
                        copy(evac_eng(), qT[:, g * 4 + m, :], ps, R=[pb], W=[bf("qT")], scale=1.0 / 16.0)
                    formW(win_v, 0, KC, C_Q + g * 512, 512, lambda kc: hT[:, kc, 2:TT + 2], [hTB], TT, eq)
                chk("proj")
                for i in range(NT):
                    pt, ptb = ptget()
                    for c in range(8):
                        P.op("pe", "transpose", pt[:, c * 128:(c + 1) * 128], k_tok[:, i, c * 128:(c + 1) * 128], ident[:],
                             R=[bf("k_tok"), bf("ident")], W=[ptb])
                    copy(evac_eng(), kT[:, :, i * 128:(i + 1) * 128], pt[:, :].rearrange("p (c n) -> p c n", n=128),
                         R=[ptb], W=[bf("kT")])
                for i in range(NT):
                    G = gates(i, True)
                    tc_ = slice(i * 128, (i + 1) * 128)
                    for h in range(4):
                        par = h % 2
                        pst, pstb = psget()
                        for j in range(2):
                            P.op("pe", "matmul", pst[:, 0:128], kT[:, 2 * h + j, tc_], qT[:, 2 * h + j, tc_],
                                 start=(j == 0), stop=(j == 1), R=[bf("kT"), bf("qT")], W=[pstb])
                        P.op("dve", "scalar_tensor_tensor", Dm[par], G["ps3"][:, h * 128:(h + 1) * 128], G["bcol"][:, h:h + 1],
                             nmask[:], ALU.subtract, ALU.min, R=[G["pb3"], G["bcb"], bf("nmask")], W=[bf(f"Dm{par}")])
                        P.op("act", "activation", Wt[par], Dm[par], AF.Exp, bias=G["li"][:, h:h + 1],
                             R=[bf(f"Dm{par}"), G["lib"]], W=[bf(f"Wt{par}")])
                        P.op("dve", "tensor_tensor", PTt[par], pst[:, 0:128], Wt[par], ALU.mult,
                             R=[pstb, bf(f"Wt{par}")], W=[bf(f"PT{par}")])
                        for j in range(2):
                            P.op("dve", "tensor_tensor", qp[par][:, j, :], qT[:, 2 * h + j, tc_], Ebc[:, h * 128:(h + 1) * 128],
                                 ALU.mult, R=[bf("qT"), bf("Ebc")], W=[bf(f"qp{par}")])
                        pnum, pnumb = psget()
                        for j in range(2):
                            P.op("pe", "matmul", pnum[:, :], qp[par][:, j, :], Cbf[:, 2 * h + j, :], start=(j == 0), stop=False,
                                 R=[bf(f"qp{par}"), bf("Cbf")], W=[pnumb])
                        P.op("pe", "matmul", pnum[:, :], PTt[par], v_tok[:, i, h * 512:(h + 1) * 512], start=False, stop=True,
                             R=[bf(f"PT{par}"), bf("v_tok")], W=[pnumb])
                        pden, pdenb = psget()
                        for j in range(2):
                            P.op("pe", "matmul", pden[:, 0:1], qp[par][:, j, :], nbf[:, 2 * h + j:2 * h + j + 1], start=(j == 0),
                                 stop=False, R=[bf(f"qp{par}"), bf("nbf")], W=[pdenb])
                        P.op("pe", "matmul", pden[:, 0:1], PTt[par], onesb[:], start=False, stop=True,
                             R=[bf(f"PT{par}"), bf("onesb")], W=[pdenb])
                        d1, d1b = col()
                        P.op("dve", "tensor_scalar", d1, pden[:, 0:1], -1.0, 1.0, ALU.mult, ALU.max, R=[pdenb], W=[d1b])
                        d2, d2b = col()
                        P.op("dve", "tensor_tensor", d2, d1, pden[:, 0:1], ALU.max, R=[d1b, pdenb], W=[d2b])
                        rd, rdb = col()
                        P.op("dve", "reciprocal", rd, d2, R=[d2b], W=[rdb])
                        P.op("act", "activation", hraw[par], pnum[:, :], AF.Copy, scale=rd, R=[pnumb, rdb], W=[bf(f"hraw{par}")])
                        ss, ssb = col()
                        P.op("act", "activation", junk[:, :], hraw[par], AF.Square, accum_out=ss,
                             R=[bf(f"hraw{par}")], W=[bf("junk"), ssb])
                        rs, rsb = rstd_from(ss, ssb, 1.0 / 512.0)
                        P.op("dve", "scalar_tensor_tensor", tq, hraw[par], rs, wbc[:, h * 512:(h + 1) * 512], ALU.mult, ALU.mult,
                             R=[bf(f"hraw{par}"), rsb, bf("wbc")], W=[bf("tq")])
                        P.op("dve", "tensor_tensor", hA_tok[:, h * 512:(h + 1) * 512], tq, sig_o[:, i, h * 512:(h + 1) * 512],
                             ALU.mult, R=[bf("tq"), bf("sig_o")], W=[bf("hA_tok")])
                    for g in range(2):
                        pt, ptb = ptget()
                        for c in range(8):
                            cc_ = g * 8 + c
                            P.op("pe", "transpose", pt[:, c * 128:(c + 1) * 128], hA_tok[:, cc_ * 128:(cc_ + 1) * 128], ident[:],
                                 R=[bf("hA_tok"), bf("ident")], W=[ptb])
                        copy(evac_eng(), hAT[:, g * 8:(g + 1) * 8, tc_], pt[:, :].rearrange("p (c n) -> p c n", n=128),
                             R=[ptb], W=[bf("hAT")])
                    state_update(i, G)
                chk("mlstm")
                P.barrier()
                for g in range(4):
                    def ecc(m, ps, pb):
                        P.op("act", "activation", ccsb[:, m, :], ps, AF.Copy, R=[pb], W=[bf(f"ccsb{m}")])
                    formW(win_v, 0, KC, C_CC + g * 512, 512, lambda kc: hT[:, kc, 0:TT + 2], [hTB], TT + 2, ecc)

                    def ecx(m, ps, pb, g=g):
                        c = g * 4 + m
                        P.op("dve", "tensor_tensor", ubuf[:, m, :], ccsb[:, m, :], ps, ALU.mult, R=[pb, bf(f"ccsb{m}")], W=[bf(f"u{m}")])
                        P.op("dve", "tensor_scalar", cacc[:, m, :], ubuf[:, m, 2:TT + 2], convw[:, 32 + c:33 + c], None, ALU.mult,
                             R=[bf(f"u{m}"), bf("convw")], W=[bf(f"cacc{m}")])
                        P.op("dve", "scalar_tensor_tensor", cacc[:, m, :], ubuf[:, m, 1:TT + 1], convw[:, 16 + c:17 + c], cacc[:, m, :],
                             ALU.mult, ALU.add, R=[bf(f"u{m}"), bf("convw"), bf(f"cacc{m}")], W=[bf(f"cacc{m}")])
                        P.op("dve", "scalar_tensor_tensor", cacc[:, m, :], ubuf[:, m, 0:TT], convw[:, c:c + 1], cacc[:, m, :],
                             ALU.mult, ALU.add, R=[bf(f"u{m}"), bf("convw"), bf(f"cacc{m}")], W=[bf(f"cacc{m}")])
                    formW(win_v, 0, KC, C_CX + g * 512, 512, lambda kc: hT[:, kc, 0:TT + 2], [hTB], TT + 2, ecx)

                    def ecb(m, ps, pb, g=g):
                        P.op("dve", "tensor_tensor", hBT[:, g * 4 + m, :], cacc[:, m, :], ps, ALU.mult,
                             R=[pb, bf(f"cacc{m}")], W=[bf("hBT")])
                    formW(win_v, 0, KC, C_CB + g * 512, 512, lambda kc: hT[:, kc, 2:TT + 2], [hTB], TT, ecb)
                chk("conv")
                P.barrier()
                for j in range(8):
                    for n in range(2):
                        def eg(m, ps, pb):
                            P.op("act", "activation", gs[:, m, :], ps, AF.Sigmoid, R=[pb], W=[bf(f"gs{m}")])
                        formW(win_v, 0, KC, C_G + n * D + j * 512, 512, lambda kc: hT[:, kc, 2:TT + 2], [hTB], TT, eg)
                        src_h, src_b = (hAT, bf("hAT")) if n == 0 else (hBT, bf("hBT"))
                        if n == 0:
                            def ey(m, ps, pb):
                                P.op("dve", "tensor_tensor", mg[:, m, :], gs[:, m, :], ps, ALU.mult, R=[pb, bf(f"gs{m}")], W=[bf(f"mg{m}")])
                        else:
                            def ey(m, ps, pb, j=j):
                                P.op("dve", "tensor_tensor", gs[:, m, :], gs[:, m, :], ps, ALU.mult, R=[pb, bf(f"gs{m}")], W=[bf(f"gs{m}")])
                                P.op("dve", "tensor_tensor", mergedT[:, j * 4 + m, :], gs[:, m, :], mg[:, m, :], ALU.add,
                                     R=[bf(f"gs{m}"), bf(f"mg{m}")], W=[bf("mergedT")])
                        formW(br_v, n * 16, 16, j * 512, 512, lambda kc, s=src_h: s[:, kc, :], [src_b], TT, ey)
                halo_copy()
                P.barrier()

                ssq, ssqb = cols[:, 256:256 + NT * 8], bf("ssq")

                def evacY(j, first=True, last=True):
                    def f(i, ps, pb):
                        yb = bf(f"Y{i}")
                        if first:
                            P.op("dve", "tensor_copy", Yv[:, i, j * 512:(j + 1) * 512], ps, R=[pb], W=[yb])
                            if last:
                                P.op("act", "activation", junk[:, :], ps, AF.Square, accum_out=ssq[:, i * 8 + j:i * 8 + j + 1],
                                     R=[pb], W=[bf("junk"), ssqb])
                        else:
                            P.op("dve", "tensor_tensor", Yv[:, i, j * 512:(j + 1) * 512], ps, Yv[:, i, j * 512:(j + 1) * 512], ALU.add,
                                 R=[pb, yb], W=[yb])
                            if last:
                                P.op("act", "activation", junk[:, :], Yv[:, i, j * 512:(j + 1) * 512], AF.Square,
                                     accum_out=ssq[:, i * 8 + j:i * 8 + j + 1], R=[yb], W=[bf("junk"), ssqb])
                    return f

                def post_norm(i):
                    s1, s1b = col()
                    P.op("dve", "tensor_reduce", s1, ssq[:, i * 8:(i + 1) * 8], AX.X, ALU.add, R=[ssqb], W=[s1b])
                    rs, rsb = rstd_from(s1, s1b, 1.0 / D)
                    P.op("dve", "scalar_tensor_tensor", Yv[:, i, :], Yv[:, i, :], rs, wbc[:], ALU.mult, ALU.mult,
                         R=[bf(f"Y{i}"), rsb, bf("wbc")], W=[bf(f"Y{i}")])

                for j in range(8):
                    formA(mo_v, 0, KC, j * 512, 512, lambda kc, i: mergedT[:, kc, i * 128:(i + 1) * 128], [bf("mergedT")], NT, evacY(j))
                P.barrier()
                load_wbc(1)
                for i in range(NT):
                    P.dma("sp", Rv[:, i, :], xl[tok0 + i * 128:tok0 + (i + 1) * 128, :], f"rin{i}", W=[bf(f"R{i}")])
                for i in range(NT):
                    post_norm(i)
                    P.op("dve", "tensor_tensor", Rv[:, i, :], Yv[:, i, :], Rv[:, i, :], ALU.add, R=[bf(f"Y{i}"), bf(f"R{i}")], W=[bf(f"R{i}")])
                    if debug:
                        P.dma("sp", dbg["dbg_x1"][tok0 + i * 128:tok0 + (i + 1) * 128, :], Rv[:, i, :], "dbg", R=[bf(f"R{i}")])
                P.barrier()

                chk("mixer")
                load_wbc(2)
                norm_T(lambda i: Rv[:, i, :], lambda i: bf(f"R{i}"), NT)
                for hh in range(4):
                    for g in range(2):
                        def eq2(m, ps, pb, g=g):
                            copy(evac_eng(), q2T[:, g * 4 + m, :], ps, R=[pb], W=[bf("q2T")])
                        formW(xq_v, 0, KC, hh * 1024 + g * 512, 512, lambda kc: hT[:, kc, 2:TT + 2], [hTB], TT, eq2)
                    for i in range(NT):
                        tc_ = slice(i * 128, (i + 1) * 128)
                        psc, pscb = psget()
                        for c in range(8):
                            P.op("pe", "matmul", psc[:, 0:256], q2T[:, c, tc_], kmT[:, hh * 8 + c, :], start=(c == 0), stop=(c == 7),
                                 R=[bf("q2T"), bf("kmT")], W=[pscb])
                        mx, mxb = col()
                        P.op("dve", "tensor_reduce", mx, psc[:, 0:256], AX.X, ALU.max, R=[pscb], W=[mxb])
                        nmx, nmxb = col()
                        P.op("dve", "tensor_scalar", nmx, mx, -1.0, None, ALU.mult, R=[mxb], W=[nmxb])
                        sm, smb = col()
                        P.op("act", "activation", p_f, psc[:, 0:256], AF.Exp, bias=nmx, accum_out=sm, R=[pscb, nmxb], W=[bf("p_f"), smb])
                        rsm, rsmb = col()
                        P.op("dve", "reciprocal", rsm, sm, R=[smb], W=[rsmb])
                        P.op("dve", "tensor_scalar", pn_b, p_f, rsm, None, ALU.mult, R=[bf("p_f"), rsmb], W=[bf("pn_b")])
                        pt, ptb = ptget()
                        for mb in range(2):
                            P.op("pe", "transpose", pt[:, mb * 128:(mb + 1) * 128], pn_b[:, mb * 128:(mb + 1) * 128], ident[:],
                                 R=[bf("pn_b"), bf("ident")], W=[ptb])
                        copy(evac_eng(), pT[:, :, tc_], pt[:, 0:256].rearrange("p (c n) -> p c n", n=128), R=[ptb], W=[bf("pT")])
                    for eb_i in range(8):
                        po, pob = psget()
                        for mb in range(2):
                            e0 = hh * 1024 + eb_i * 128
                            P.op("pe", "matmul", po[:, 0:TT], vm[:, mb, e0:e0 + 128], pT[:, mb, :], start=(mb == 0), stop=(mb == 1),
                                 R=[bf("vm"), bf("pT")], W=[pob])
                        copy(evac_eng(), oT[:, hh * 8 + eb_i, :], po[:, 0:TT], R=[pob], W=[bf("oT")])
                P.barrier()
                for j in range(8):
                    formA(xo_v, 0, KC, j * 512, 512, lambda kc, i: oT[:, kc, i * 128:(i + 1) * 128], [bf("oT")], NT, evacY(j))
                load_wbc(3)
                for i in range(NT):
                    post_norm(i)
                    P.op("dve", "tensor_tensor", Rv[:, i, :], Yv[:, i, :], Rv[:, i, :], ALU.add, R=[bf(f"Y{i}"), bf(f"R{i}")], W=[bf(f"R{i}")])
                    if debug:
                        P.dma("sp", dbg["dbg_x2"][tok0 + i * 128:tok0 + (i + 1) * 128, :], Rv[:, i, :], "dbg", R=[bf(f"R{i}")])
                P.barrier()

                chk("xattn")
                load_wbc(5)
                norm_T(lambda i: Rv[:, i, :], lambda i: bf(f"R{i}"), NT)
                parts = [(c0, min(8, FC - c0)) for c0 in range(0, FC, 8)]
                for pi, (c0, nch) in enumerate(parts):
                    ap_ = aTp[pi % 2]
                    apb = bf(f"aT{pi % 2}")
                    for g0 in range(0, nch, 4):
                        nm_ = min(4, nch - g0)
                        sg_ = sgt[(g0 // 4) % 2]
                        sgb = bf(f"sg{(g0 // 4) % 2}")

                        def egt(m, ps, pb, sg_=sg_, sgb=sgb):
                            P.op("act", "activation", sg_[:, m, :], ps, AF.Silu, R=[pb], W=[sgb])

                        def eup(m, ps, pb, sg_=sg_, sgb=sgb, g0=g0, ap_=ap_, apb=apb):
                            P.op("dve", "tensor_tensor", ap_[:, g0 + m, :], sg_[:, m, :], ps, ALU.mult, R=[pb, sgb], W=[apb])
                        formW(fg_v, 0, KC, (c0 + g0) * 128, nm_ * 128, lambda kc: hT[:, kc, 2:TT + 2], [hTB], TT, egt)
                        formW(fu_v, 0, KC, (c0 + g0) * 128, nm_ * 128, lambda kc: hT[:, kc, 2:TT + 2], [hTB], TT, eup)
                    for j in range(8):
                        formA(fd_v, c0, nch, j * 512, 512, lambda kc, i, ap_=ap_: ap_[:, kc, i * 128:(i + 1) * 128], [apb], NT,
                              evacY(j, first=(pi == 0), last=(pi == len(parts) - 1)))
                load_wbc(6)
                for i in range(NT):
                    post_norm(i)
                    P.op("dve", "tensor_tensor", Yv[:, i, :], Yv[:, i, :], Rv[:, i, :], ALU.add, R=[bf(f"Y{i}"), bf(f"R{i}")], W=[bf(f"Y{i}")])
                    od = P.dma("sp", out[tok0 + i * 128:tok0 + (i + 1) * 128, :], Yv[:, i, :], "out", R=[bf(f"Y{i}")])
                    out_dmas.append(od)
            return out_dmas

        plan = []
        wcache = {"idx": {}, "t": None, "pre": []}
        schedule(Prog(True), plan, 0)
        cnt = {}
        for (_, _, _, key) in plan:
            cnt[key] = cnt.get(key, 0) + 1
        for (_, _, _, key) in plan:
            if cnt[key] > 1 and key not in wcache["idx"]:
                wcache["idx"][key] = len(wcache["idx"])
        seen_pre = set(k for (_, _, _, k) in plan[:wcache.get("local0", 0)])
        pre = []
        for ent in plan[wcache.get("local0", len(plan)):]:
            k = ent[3]
            if k in wcache["idx"] and k not in seen_pre:
                seen_pre.add(k)
                pre.append(ent)
        wcache["pre"] = pre[:int(0.6 * wcache.get("local0", 0))] if NPRE >= 8 else pre[:8]
        if wcache["idx"]:
            nt_ = len(wcache["idx"])
            chunks = [nc.dram_tensor(f"wcache{c}", [min(192, nt_ - c * 192), 128, 4096], BF16, kind="Internal").ap()
                      for c in range((nt_ + 191) // 192)]

            class _Cache:
                def __getitem__(self, i):
                    return chunks[i // 192][i % 192]
            wcache["t"] = _Cache()
        P = Prog(False)
        outs = schedule(P, plan, len(plan))
        fw = {"sp": [outs[-1]] + ([P.last_dma["dbg"]] if debug else [])}
        P.emit(nc, st, fw)
    return nc


def _host_params(inp):
    f = lambda a: np.ascontiguousarray(np.asarray(a, dtype=np.float32))
    nws = [inp["norm_pre_mix"][0], inp["norm_post_mix"][0], inp["norm_pre_xattn"][0], inp["norm_post_xattn"][0],
           inp["norm_mem"][0], inp["norm_pre_ffn"][0], inp["norm_post_ffn"][0]]
    nw_bc = f(np.broadcast_to(np.stack([np.asarray(v) for v in nws])[:, None, :], (7, 128, D)))
    hnw_bc = f(np.broadcast_to(np.asarray(inp["mlstm_head_norm"][0])[None, :], (128, 2048)))
    bif_bc = f(np.broadcast_to(np.asarray(inp["b_if"][0]).reshape(1, 8), (128, 8)))
    cw = np.asarray(inp["conv_w"][0])
    convw_c = f(cw.reshape(3, 16, 128).transpose(2, 0, 1).reshape(128, 48))
    return dict(
        w_in=f(inp["w_in"][0]), w_br=f(np.asarray(inp["w_branch"][0]).reshape(D, D)), w_mo=f(inp["w_mix_out"][0]),
        w_xq=f(inp["w_xq"][0]), w_xk=f(inp["w_xk"][0]), w_xv=f(inp["w_xv"][0]), w_xo=f(inp["w_xo"][0]),
        w_fg=f(inp["w_ffn_gate"][0]), w_fu=f(inp["w_ffn_up"][0]), w_fd=f(inp["w_ffn_down"][0]),
        nw_bc=nw_bc, hnw_bc=hnw_bc, bif_bc=bif_bc, convw_c=convw_c)


def kernel(**inputs):
    x = np.asarray(inputs["x"], dtype=np.float32)
    mem = np.asarray(inputs["mem"], dtype=np.float32)
    Bsz, S, _ = x.shape
    NSEG = 8 // Bsz
    SEG = S // NSEG
    NLOC = SEG // 128
    NPRE = (S - SEG) // 128
    common = _host_params(inputs)
    nc = build(NLOC, NPRE, 2)
    in_maps = []
    for c in range(8):
        b, s = c // NSEG, c % NSEG
        xpre = np.zeros((NPRE * 128, D), np.float32)
        if s > 0:
            xpre[NPRE * 128 - s * SEG:] = x[b, :s * SEG]
        m = dict(common)
        m["xl"] = np.ascontiguousarray(x[b, s * SEG:(s + 1) * SEG])
        m["xp"] = xpre
        m["memb"] = np.ascontiguousarray(mem[b])
        in_maps.append(m)
    res = run_bass_kernel_spmd(nc, in_maps, core_ids=list(range(8)))
    outp = np.empty((Bsz, S, D), np.float32)
    for c in range(8):
        b, s = c // NSEG, c % NSEG
        outp[b, s * SEG:(s + 1) * SEG] = res.results[c]["out"]
    return outp
```
